# Optimizing a Trainium2 kernel written in Bass

```python
import jax, jax.numpy as jnp
from jax import lax
import numpy as np

D_MODEL = 1024
BATCH = 8
SEQ = 2048
DEPTH = 1
DEC_BATCH = 128
DEC_SEQ = 8
PAST_LEN = 16384
PAGE_SIZE = 128

D_CONF = D_MODEL
CONF_K = 31
D_RNN = 5 * D_MODEL // 4
RNN_BLOCKS = 8
RNN_BLK = D_RNN // RNN_BLOCKS
RNN_K = 4
LRU_C = 8.0
D_FF = 3 * D_MODEL
FFN_K = 3
N_MOD = 6
EPS = 1e-6
IN_SPLITS = [D_CONF, 2 * D_CONF, 2 * D_CONF + D_RNN, 2 * D_CONF + 2 * D_RNN, 2 * D_CONF + 2 * D_RNN + D_MODEL]
D_IN = 2 * D_CONF + 2 * D_RNN + 2 * D_MODEL

kernel_name = "gated_parallel_conformer_rglru_convffn_step"


def _rmsnorm(x, g):
    xf = x.astype(jnp.float32)
    y = xf * lax.rsqrt(jnp.mean(xf * xf, axis=-1, keepdims=True) + EPS) * g.astype(jnp.float32)
    return y.astype(x.dtype)


def _layernorm(x, g, b):
    xf = x.astype(jnp.float32)
    mu = jnp.mean(xf, axis=-1, keepdims=True)
    var = jnp.mean(jnp.square(xf - mu), axis=-1, keepdims=True)
    y = (xf - mu) * lax.rsqrt(var + EPS) * g.astype(jnp.float32) + b.astype(jnp.float32)
    return y.astype(x.dtype)


def _causal_dwconv(buf, u, w, b):
    k = w.shape[0]
    t = u.shape[1]
    full = jnp.concatenate([buf.astype(u.dtype), u], axis=1)
    out = full[:, 0:t] * w[0]
    for j in range(1, k):
        out = out + full[:, j:j + t] * w[j]
    return out + b, full[:, t:]


def _linear_scan(a, bx, h0):
    def step(h, ab):
        a_t, b_t = ab
        h = a_t * h + b_t
        return h, h
    h_last, hs = lax.scan(step, h0, (jnp.swapaxes(a, 0, 1), jnp.swapaxes(bx, 0, 1)))
    return jnp.swapaxes(hs, 0, 1), h_last


def _layer(x, c, conf_buf, rconv_buf, h0, ffn_buf, p):
    bsz, t, _ = x.shape
    mod = jax.nn.silu(c) @ p["w_ada"] + p["b_ada"]
    shift1, scale1, gate1, shift2, scale2, gate2 = [m[:, None, :] for m in jnp.split(mod, N_MOD, axis=-1)]

    h = _rmsnorm(x, p["g_norm1"]) * (1 + scale1) + shift1
    z = h @ p["w_in"]
    conf_v, conf_g, rnn_x, rnn_gate, mg_a, mg_b = jnp.split(z, IN_SPLITS, axis=-1)

    u = conf_v * jax.nn.sigmoid(conf_g)
    u, new_conf_buf = _causal_dwconv(conf_buf, u, p["w_conf_dw"], p["b_conf_dw"])
    u = jax.nn.silu(_layernorm(u, p["g_conf_ln"], p["b_conf_ln"]))
    y_a = u @ p["w_conf_out"]

    xr, new_rconv_buf = _causal_dwconv(rconv_buf, rnn_x, p["w_rnn_conv"], p["b_rnn_conv"])
    xb = xr.reshape(bsz, t, RNN_BLOCKS, RNN_BLK)
    r = jax.nn.sigmoid((jnp.einsum("btgi,gij->btgj", xb, p["w_rg"]).reshape(bsz, t, D_RNN) + p["b_rg"]).astype(jnp.float32))
    ig = jax.nn.sigmoid((jnp.einsum("btgi,gij->btgj", xb, p["w_ig"]).reshape(bsz, t, D_RNN) + p["b_ig"]).astype(jnp.float32))
    log_a = -LRU_C * r * jax.nn.softplus(-p["lru_lambda"].astype(jnp.float32))
    a = jnp.exp(log_a)
    mult = jnp.sqrt(-jnp.expm1(2.0 * log_a))
    bx = mult * ig * xr.astype(jnp.float32)
    hs, h_last = _linear_scan(a, bx, h0.astype(jnp.float32))
    y_b = (hs.astype(x.dtype) * jax.nn.gelu(rnn_gate)) @ p["w_rnn_out"]

    merged = jax.nn.sigmoid(mg_a) * y_a + jax.nn.sigmoid(mg_b) * y_b
    x = x + gate1 * (merged @ p["w_out"])

    h = _rmsnorm(x, p["g_norm2"]) * (1 + scale2) + shift2
    up = h @ p["w_up"]
    up, new_ffn_buf = _causal_dwconv(ffn_buf, up, p["w_ffn_dw"], p["b_ffn_dw"])
    g, v = jnp.split(up, 2, axis=-1)
    x = x + gate2 * ((jax.nn.gelu(g) * v) @ p["w_down"])
    return x, new_conf_buf, new_rconv_buf, h_last.astype(x.dtype), new_ffn_buf


def setup_inputs(seed: int = 0) -> dict:
    key = jax.random.key(seed)
    ks = jax.random.split(key, 40)
    f32 = jnp.float32
    nrm = lambda k, shape, scale: jax.random.normal(k, shape, f32) * scale
    L = DEPTH
    u_a = jax.random.uniform(ks[0], (L, D_RNN), f32, 0.9, 0.999)
    s = u_a ** (1.0 / LRU_C)
    lru_lambda = jnp.log(s / (1.0 - s))
    return {
        "x_prompt": nrm(ks[1], (BATCH, SEQ, D_MODEL), 1.0),
        "x_sample": nrm(ks[2], (DEC_BATCH, DEC_SEQ, D_MODEL), 1.0),
        "state_conf_conv": nrm(ks[3], (L, DEC_BATCH, CONF_K - 1, D_CONF), 0.5),
        "state_rnn_conv": nrm(ks[4], (L, DEC_BATCH, RNN_K - 1, D_RNN), 0.5),
        "state_rnn_h": nrm(ks[5], (L, DEC_BATCH, D_RNN), 0.5),
        "state_ffn_conv": nrm(ks[6], (L, DEC_BATCH, FFN_K - 1, 2 * D_FF), 0.5),
        "c_prompt": nrm(ks[7], (BATCH, D_MODEL), 1.0),
        "c_sample": nrm(ks[8], (DEC_BATCH, D_MODEL), 1.0),
        "w_ada": nrm(ks[9], (L, D_MODEL, N_MOD * D_MODEL), 0.5 * D_MODEL ** -0.5),
        "b_ada": nrm(ks[10], (L, N_MOD * D_MODEL), 0.02),
        "g_norm1": 1.0 + nrm(ks[11], (L, D_MODEL), 0.02),
        "w_in": nrm(ks[12], (L, D_MODEL, D_IN), D_MODEL ** -0.5),
        "w_conf_dw": nrm(ks[13], (L, CONF_K, D_CONF), CONF_K ** -0.5),
        "b_conf_dw": nrm(ks[14], (L, D_CONF), 0.02),
        "g_conf_ln": 1.0 + nrm(ks[15], (L, D_CONF), 0.02),
        "b_conf_ln": nrm(ks[16], (L, D_CONF), 0.02),
        "w_conf_out": nrm(ks[17], (L, D_CONF, D_MODEL), D_CONF ** -0.5),
        "w_rnn_conv": nrm(ks[18], (L, RNN_K, D_RNN), RNN_K ** -0.5),
        "b_rnn_conv": nrm(ks[19], (L, D_RNN), 0.02),
        "w_rg": nrm(ks[20], (L, RNN_BLOCKS, RNN_BLK, RNN_BLK), RNN_BLK ** -0.5),
        "b_rg": nrm(ks[21], (L, D_RNN), 0.02),
        "w_ig": nrm(ks[22], (L, RNN_BLOCKS, RNN_BLK, RNN_BLK), RNN_BLK ** -0.5),
        "b_ig": nrm(ks[23], (L, D_RNN), 0.02),
        "lru_lambda": lru_lambda,
        "w_rnn_out": nrm(ks[24], (L, D_RNN, D_MODEL), D_RNN ** -0.5),
        "w_out": nrm(ks[25], (L, D_MODEL, D_MODEL), D_MODEL ** -0.5),
        "g_norm2": 1.0 + nrm(ks[26], (L, D_MODEL), 0.02),
        "w_up": nrm(ks[27], (L, D_MODEL, 2 * D_FF), D_MODEL ** -0.5),
        "w_ffn_dw": nrm(ks[28], (L, FFN_K, 2 * D_FF), FFN_K ** -0.5),
        "b_ffn_dw": nrm(ks[29], (L, 2 * D_FF), 0.02),
        "w_down": nrm(ks[30], (L, D_FF, D_MODEL), D_FF ** -0.5),
        "g_final": 1.0 + nrm(ks[31], (D_MODEL,), 0.02),
    }


def reference(x_prompt, x_sample, state_conf_conv, state_rnn_conv, state_rnn_h, state_ffn_conv,
              c_prompt, c_sample, w_ada, b_ada, g_norm1, w_in, w_conf_dw, b_conf_dw, g_conf_ln,
              b_conf_ln, w_conf_out, w_rnn_conv, b_rnn_conv, w_rg, b_rg, w_ig, b_ig, lru_lambda,
              w_rnn_out, w_out, g_norm2, w_up, w_ffn_dw, b_ffn_dw, w_down, g_final):
    bp = x_prompt.shape[0]
    dt = x_prompt.dtype
    xp, xs = x_prompt, x_sample
    conf_p, rconv_p, h_p, ffn_p = [], [], [], []
    conf_s, rconv_s, h_s, ffn_s = [], [], [], []
    for l in range(DEPTH):
        p = {"w_ada": w_ada[l], "b_ada": b_ada[l], "g_norm1": g_norm1[l], "w_in": w_in[l],
             "w_conf_dw": w_conf_dw[l], "b_conf_dw": b_conf_dw[l], "g_conf_ln": g_conf_ln[l],
             "b_conf_ln": b_conf_ln[l], "w_conf_out": w_conf_out[l], "w_rnn_conv": w_rnn_conv[l],
             "b_rnn_conv": b_rnn_conv[l], "w_rg": w_rg[l], "b_rg": b_rg[l], "w_ig": w_ig[l],
             "b_ig": b_ig[l], "lru_lambda": lru_lambda[l], "w_rnn_out": w_rnn_out[l],
             "w_out": w_out[l], "g_norm2": g_norm2[l], "w_up": w_up[l], "w_ffn_dw": w_ffn_dw[l],
             "b_ffn_dw": b_ffn_dw[l], "w_down": w_down[l]}
        xp, cb, rb, hl, fb = _layer(
            xp, c_prompt,
            jnp.zeros((bp, CONF_K - 1, D_CONF), dt), jnp.zeros((bp, RNN_K - 1, D_RNN), dt),
            jnp.zeros((bp, D_RNN), dt), jnp.zeros((bp, FFN_K - 1, 2 * D_FF), dt), p)
        conf_p.append(cb); rconv_p.append(rb); h_p.append(hl); ffn_p.append(fb)
        xs, cb, rb, hl, fb = _layer(
            xs, c_sample, state_conf_conv[l], state_rnn_conv[l], state_rnn_h[l], state_ffn_conv[l], p)
        conf_s.append(cb); rconv_s.append(rb); h_s.append(hl); ffn_s.append(fb)
    y_prompt = _rmsnorm(xp, g_final)
    y_sample = _rmsnorm(xs, g_final)
    return (y_prompt, y_sample,
            jnp.stack(conf_p), jnp.stack(rconv_p), jnp.stack(h_p), jnp.stack(ffn_p),
            jnp.stack(conf_s), jnp.stack(rconv_s), jnp.stack(h_s), jnp.stack(ffn_s))
```

```python
import contextlib
import numpy as np
import concourse.bass as bass
import concourse.mybir as mybir
from concourse.bass_utils import run_bass_kernel_spmd

F32 = mybir.dt.float32
BF16 = mybir.dt.bfloat16
AF = mybir.ActivationFunctionType
ALU = mybir.AluOpType

NCORES = 8
D = 1024
T = 2048
NS = 16
TS = 8
DR = 1280
DFF = 3072
NPASS = 2
TP = T // NPASS
NCMAX = TP + NS * TS
EPS = 1e-6
KC = 31
GF = 4

COMPUTE = ("pe", "act", "dve", "pool")

PVO = {}
_o = 0
for _n, _w in [("g1", 8), ("g2", 8), ("gf", 8), ("bcd", 8), ("gln", 8), ("bln", 8), ("wcd", 8 * 31),
               ("brc", 10), ("wrc", 40), ("brg", 10), ("big", 10), ("lam", 10), ("bfd", 48), ("wfd", 144),
               ("bada", 48)]:
    PVO[_n] = _o
    _o += _w
NPV = _o

STO = {}
_o = 0
for _n, _w in [("conf_p", 8 * 30), ("rconv_p", 10 * 3), ("h_p", 10), ("ffn_p", 48 * 2),
               ("rconv_s", 10 * 16 * 3), ("h_s", 10 * 16), ("ffn_s", 48 * 16 * 2)]:
    STO[_n] = (_o, _w)
    _o += _w
NST = _o


def gate_klist():
    out = []
    for j in range(10):
        lo, hi = 128 * j, 128 * j + 127
        g_lo, g_hi = lo // 160, hi // 160
        k_lo, k_hi = (160 * g_lo) // 128, (160 * g_hi + 159) // 128
        out.append(list(range(k_lo, k_hi + 1)))
    return out


KLIST = gate_klist()


class Op:
    __slots__ = ("eng", "fn", "raw", "oth", "kind", "sig", "seq", "dsem", "dval", "deps")

    def __init__(self, eng, fn, raw, oth, kind):
        self.eng, self.fn, self.raw, self.oth, self.kind = eng, fn, raw, oth, kind
        self.sig = False
        self.seq = 0
        self.dsem = None
        self.dval = 0
        self.deps = ()


class Prog:
    def __init__(self, nc, n_dma_sems=10, dry=False):
        self.nc = nc
        self.ops = []
        self.lastw = {}
        self.readers = {}
        self.fence_deps = set()
        self.last_fence_at = -1
        self.n_dma_sems = n_dma_sems
        self.dry = dry

    def op(self, eng, fn, reads=(), writes=(), kind="c"):
        if self.dry:
            return -1
        raw = set()
        oth = set()
        for k in reads:
            w = self.lastw.get(k)
            if w is not None:
                raw.add(w)
        for k in writes:
            w = self.lastw.get(k)
            if w is not None:
                oth.add(w)
            oth.update(self.readers.get(k, ()))
        raw.update(self.fence_deps)
        i = len(self.ops)
        self.ops.append(Op(eng, fn, raw, oth, kind))
        for k in reads:
            self.readers.setdefault(k, []).append(i)
        for k in writes:
            self.lastw[k] = i
            self.readers[k] = []
        return i

    def dma(self, eng, out, in_, reads=(), writes=()):
        return self.op(eng, lambda e: e.dma_start(out=out, in_=in_), reads, writes, kind="dma")

    def fence(self):
        if self.dry:
            return
        deps = set()
        last = {}
        for i, o in enumerate(self.ops):
            if o.kind == "dma":
                if i > self.last_fence_at:
                    deps.add(i)
            else:
                last[o.eng] = i
        deps.update(last.values())
        self.last_fence_at = len(self.ops) - 1
        self.fence_deps = deps

    def emit(self):
        nc = self.nc
        ops = self.ops
        for o in ops:
            deps = set()
            for d in o.raw:
                p = ops[d]
                if p.kind != "dma" and o.kind != "dma" and p.eng == o.eng and o.eng == "pe":
                    continue
                deps.add(d)
            for d in o.oth:
                p = ops[d]
                if p.kind != "dma" and o.kind != "dma" and p.eng == o.eng:
                    continue
                deps.add(d)
            o.deps = deps
            for d in deps:
                ops[d].sig = True
        engs = {"pe": nc.tensor, "act": nc.scalar, "dve": nc.vector, "pool": nc.gpsimd, "sp": nc.sync}
        with contextlib.ExitStack() as st:
            sems = {e: st.enter_context(nc.semaphore("s_" + e)) for e in COMPUTE}
            dsems = {q: [st.enter_context(nc.semaphore("d_%s%d" % (q, j))) for j in range(self.n_dma_sems)]
                     for q in ("sp", "pool")}
            cnt = {e: 0 for e in COMPUTE}
            dcnt = {q: 0 for q in dsems}
            for o in ops:
                if o.kind == "dma":
                    n = dcnt[o.eng]
                    dcnt[o.eng] = n + 1
                    o.dsem = dsems[o.eng][n % self.n_dma_sems]
                    o.dval = 16 * (n // self.n_dma_sems + 1)
                elif o.sig:
                    cnt[o.eng] += 1
                    o.seq = cnt[o.eng]
            per_eng = {e: [] for e in engs}
            for o in ops:
                per_eng[o.eng].append(o)

            def ticket(o):
                if o.kind == "dma":
                    return o.dsem, o.dval
                return sems[o.eng], o.seq

            def run(ename, eobj):
                waited = {}
                for o in per_eng[ename]:
                    ws = {}
                    for d in o.deps:
                        s, v = ticket(ops[d])
                        if ws.get(s, 0) < v:
                            ws[s] = v
                    if o.kind == "dma" and o.dval > 16:
                        if ws.get(o.dsem, 0) < o.dval - 16:
                            ws[o.dsem] = o.dval - 16
                    for s, v in ws.items():
                        if waited.get(s, 0) >= v:
                            continue
                        eobj.wait_ge(s, v)
                        waited[s] = v
                    ins = o.fn(eobj)
                    if o.kind == "dma":
                        ins.then_inc(o.dsem, 16)
                    elif o.sig:
                        ins.then_inc(sems[o.eng], 1)
                if ename in dsems:
                    last = {}
                    for o in per_eng[ename]:
                        if o.kind == "dma":
                            last[o.dsem] = o.dval
                    for s, v in last.items():
                        if waited.get(s, 0) < v:
                            eobj.wait_ge(s, v)

            with nc.Block() as block:
                @block.tensor
                def _(e):
                    run("pe", e)

                @block.scalar
                def _(e):
                    run("act", e)

                @block.vector
                def _(e):
                    run("dve", e)

                @block.gpsimd
                def _(e):
                    run("pool", e)

                @block.sync
                def _(e):
                    run("sp", e)


class CT:
    def __init__(self, idx, c0, cn, kind):
        self.idx, self.c0, self.cn, self.kind = idx, c0, cn, kind
        self.s = kind == "s"

    def v(self, ap2):
        if self.s:
            return ap2.rearrange("p (s t) -> p s t", t=TS)
        return ap2

    def flat(self, buf, ch=None):
        a = buf[:, self.c0:self.c0 + self.cn] if ch is None else buf[:, ch, self.c0:self.c0 + self.cn]
        return self.v(a)

    def flat2(self, buf, ch=None):
        return buf[:, self.c0:self.c0 + self.cn] if ch is None else buf[:, ch, self.c0:self.c0 + self.cn]

    def tmp(self, t):
        return self.v(t[:, 0:self.cn])

    def tmp2(self, t):
        return t[:, 0:self.cn]

    def _sv(self, buf, ch, h):
        pb = h + TP
        return buf[:, ch, pb:pb + NS * (h + TS)].rearrange("p (s w) -> p s w", w=h + TS)

    def halo_w(self, buf, ch, h):
        if self.s:
            return self._sv(buf, ch, h)[:, :, h:h + TS]
        return buf[:, ch, h + self.c0:h + self.c0 + self.cn]

    def halo_r(self, buf, ch, h, j):
        if self.s:
            return self._sv(buf, ch, h)[:, :, j:j + TS]
        return buf[:, ch, self.c0 + j:self.c0 + j + self.cn]

    def k_w(self, name, ch, h):
        if self.s:
            return [(name, ch, "s")]
        a, b = (h + self.c0) // 512, (h + self.c0 + self.cn - 1) // 512
        return [(name, ch, q) for q in range(a, b + 1)]

    def k_r(self, name, ch, h):
        if self.s:
            return [(name, ch, "s")]
        a, b = self.c0 // 512, (self.c0 + self.cn + h - 1) // 512
        return [(name, ch, q) for q in range(a, b + 1)]

    def k_s(self, name, ch):
        if self.s:
            return [(name, ch, "s")]
        return [(name, ch, self.c0 // 512)]


def pass_tiles(p):
    tiles = [CT(0, 0, 512, "p"), CT(1, 512, 512, "p")]
    if p == 0:
        tiles.append(CT(2, TP, NS * TS, "s"))
    return tiles


def build_program(nc, P, wplan, dram):
    A = {}
    off = [16640]

    def alloc(name, shape, dt):
        nb = int(np.prod(shape[1:])) * (4 if dt == F32 else 2)
        nb = (nb + 63) // 64 * 64
        t = nc.alloc_sbuf_tensor_at(name, list(shape), dt, offset=off[0])
        off[0] += nb
        assert off[0] <= 229376, (name, off[0])
        A[name] = t.ap()
        return A[name]

    PV = alloc("PV", [128, NPV], F32)
    CNEG = alloc("CNEG", [128, 20], F32)
    HB = alloc("HB", [128, 20], F32)
    scr0 = off[0]
    IDF = alloc("IDF", [128, 128], F32)
    CTs = alloc("CTs", [128, 8, 17], F32)
    scr1 = off[0]
    off[0] = scr0
    DG4 = alloc("DG4", [128, 4, 128], BF16)
    assert off[0] <= scr1
    off[0] = scr1
    IDB = alloc("IDB", [128, 128], BF16)
    ONES = alloc("ONES", [128, 128], BF16)
    CTb_ = alloc("CTb", [128, 8, 17], BF16)
    alloc_late = lambda n, sh, dt: CTb_
    MOD = alloc("MOD", [128, 48, 17], F32)
    X = alloc("X", [128, 8, NCMAX], F32)
    H = alloc("H", [128, 8, NCMAX], BF16)
    YA = alloc("YA", [128, 8, NCMAX], BF16)
    ST = alloc("ST", [128, NST], F32)
    UH = alloc("UH", [128, 8, 30], BF16)
    RXH = alloc("RXH", [128, 10, 3], BF16)
    UPH = alloc("UPH", [128, 48, 2], BF16)
    HCAR = alloc("HCAR", [128, 10], F32)
    NWS, NWB = 3, 4
    WS = alloc("WS", [128, NWS, 8 * 128], F32)
    WB = alloc("WB", [128, NWB, 8 * 128], BF16)
    STT_ = [alloc("STAT%d" % i, [128, 512], F32) for i in range(3)]
    GT_ = [alloc("GT%d" % i, [128, 512], F32) for i in range(3)]
    FX = alloc("FX", [128, 16], F32)
    tbn = [0]
    stn = [0]

    def nexttb():
        tbn[0] += 1
        i = tbn[0] % 3
        return GT_[i], "GT%d" % i

    def nextstat():
        stn[0] += 1
        i = stn[0] % 3
        return STT_[i], "STAT%d" % i
    SQ = alloc("SQ", [128, 2, 512], BF16)
    EPSB = alloc("EPSB", [128, 1], F32)
    region = off[0]
    UC = alloc("UC", [128, 8, 30 + TP + NS * 38], BF16)
    DG = alloc("DG", [128, 2, KC, 128], BF16)
    CS32 = alloc("CS32", [128, 8, NS, 30], F32)
    CSO = alloc("CSO", [128, 8, NS, 30], F32)
    endA = off[0]
    off[0] = region
    RX = alloc("RX", [128, 10, 3 + TP + NS * 11], BF16)
    HG = alloc("HG", [128, 10, NCMAX], BF16)
    BT = [alloc("BT%d" % i, [128, NCMAX], F32) for i in range(8)]
    endB = off[0]
    off[0] = region
    UPB = alloc("UPB", [128, 4 * GF, 2 + TP + NS * 10], BF16)
    GV = alloc("GV", [128, 2 * GF, NCMAX], BF16)
    FT = [alloc("FT%d" % i, [128, 512], F32) for i in range(4)]
    DG3 = alloc("DG3", [128, 2, 6, 128], BF16)
    endF = off[0]
    off[0] = max(endA, endB, endF) - 8 * TP * 4
    assert off[0] >= region + 42496
    XN = alloc("XN", [128, 8, TP], F32)

    PSB = [nc.alloc_psum_tensor("psb%d" % i, [128, 512], F32).ap() for i in range(8)]
    psn = [0]

    def psum():
        b = psn[0] % 8
        psn[0] += 1
        return b

    def pv(name, i, w=1):
        o = PVO[name] + i
        return PV[:, o:o + w]

    def st(name):
        o, w = STO[name]
        return ST[:, o:o + w]

    cast_mode = ["act"]
    wreq = []
    wstate = {"emitted": 0}
    PF = 3

    def wload(m):
        key, kc, cast = wplan[m]
        src = dram[key[0]][key[1]] if len(key) == 2 else dram[key[0]][key[1], key[2]]
        ss = m % NWS
        P.dma("sp", WS[:, ss, 0:kc * 128], src, writes=[("ws", ss)])
        if cast:
            sb = m % NWB
            if (m % 2 == 0 and cast_mode[0] != "pool") or cast_mode[0] == "act":
                P.op("act", lambda e, ss=ss, sb=sb, kc=kc: e.activation(out=WB[:, sb, 0:kc * 128], in_=WS[:, ss, 0:kc * 128], func=AF.Copy),
                     reads=[("ws", ss)], writes=[("wb", sb)])
            else:
                P.op("pool", lambda e, ss=ss, sb=sb, kc=kc: e.tensor_copy(out=WB[:, sb, 0:kc * 128], in_=WS[:, ss, 0:kc * 128]),
                     reads=[("ws", ss)], writes=[("wb", sb)])

    def wget(key, kc, cast=True):
        if wplan is None:
            wreq.append((key, kc, cast))
            return None, None
        m = wstate.setdefault("next", 0)
        wstate["next"] = m + 1
        assert wplan[m] == (key, kc, cast), (m, wplan[m], key)
        while wstate["emitted"] < min(len(wplan), m + PF):
            wload(wstate["emitted"])
            wstate["emitted"] += 1
        if cast:
            sb = m % NWB
            return WB[:, sb, 0:kc * 128].rearrange("p (k f) -> p k f", f=128), ("wb", sb)
        ss = m % NWS
        return WS[:, ss, 0:kc * 128].rearrange("p (k f) -> p k f", f=128), ("ws", ss)

    def proj(key, kc, rhs_fn, rhs_keys_fn, tiles):
        w, wk = wget(key, kc)
        outs = []
        for ct in tiles:
            b = psum()
            if w is not None:
                for k in range(kc):
                    P.op("pe", lambda e, b=b, k=k, ct=ct, w=w: e.matmul(PSB[b][:, 0:ct.cn], lhsT=w[:, k, :], rhs=rhs_fn(k, ct),
                                                                      start=(k == 0), stop=(k == kc - 1)),
                         reads=[wk] + rhs_keys_fn(k, ct), writes=[("ps", b)])
            outs.append((ct, b))
        return outs

    def proj_multi(slabs, rhs_fn, rhs_keys_fn, tiles):
        banks = [psum() for _ in tiles]
        nk = sum(sl[1] for sl in slabs)
        done = 0
        for (key, kc, koff) in slabs:
            w, wk = wget(key, kc)
            if w is not None:
                for ct, b in zip(tiles, banks):
                    for k in range(kc):
                        P.op("pe", lambda e, b=b, k=k, ct=ct, w=w, koff=koff, first=(done + k == 0), lastk=(done + k == nk - 1):
                             e.matmul(PSB[b][:, 0:ct.cn], lhsT=w[:, k, :], rhs=rhs_fn(koff + k, ct), start=first, stop=lastk),
                             reads=[wk] + rhs_keys_fn(koff + k, ct), writes=[("ps", b)])
            done += kc
        return list(zip(tiles, banks))

    def psv(ct, b):
        return ct.v(PSB[b][:, 0:ct.cn])

    def modp(ct, sec, ch):
        if ct.s:
            return MOD[:, sec * 8 + ch, 0:16].unsqueeze(2).to_broadcast([128, NS, TS])
        return MOD[:, sec * 8 + ch, 16:17]

    P.dma("pool", PV, dram["pv"], writes=["PV"])
    P.dma("pool", IDF, dram["idn"], writes=["IDF"])
    P.dma("pool", CTs, dram["cT"], writes=["CTs"])
    P.dma("pool", st("ffn_s"), dram["fs_in"], writes=[("ST", "ffn_s")])
    P.dma("pool", st("rconv_s"), dram["rs_in"], writes=[("ST", "rconv_s")])
    P.dma("pool", st("h_s"), dram["h_in"], writes=[("ST", "h_s")])
    P.op("dve", lambda e: e.tensor_copy(out=IDB, in_=IDF), reads=["IDF"], writes=["IDB"])
    P.op("pool", lambda e: e.memset(ONES, 1.0), writes=["ONES"])
    P.op("pool", lambda e: e.memset(HCAR, 0.0), writes=[("HCAR", j) for j in range(10)])
    P.op("pool", lambda e: e.memset(EPSB, EPS), writes=["EPSB"])
    P.op("act", lambda e: e.activation(out=CNEG[:, 0:10], in_=pv("lam", 0, 10), func=AF.Exp, scale=-1.0), reads=["PV"], writes=["CNEG"])
    P.op("act", lambda e: e.activation(out=CNEG[:, 0:10], in_=CNEG[:, 0:10], func=AF.Ln, bias=1.0), reads=["CNEG"], writes=["CNEG"])
    P.op("dve", lambda e: e.tensor_scalar(out=CNEG[:, 10:20], in0=CNEG[:, 0:10], scalar1=-8.0, scalar2=None, op0=ALU.mult), reads=["CNEG"], writes=["CNEG2"])
    P.op("dve", lambda e: e.tensor_scalar(out=CNEG[:, 0:10], in0=CNEG[:, 0:10], scalar1=-4.0, scalar2=None, op0=ALU.mult), reads=["CNEG", "CNEG2"], writes=["CNEG"])
    P.op("dve", lambda e: e.tensor_scalar(out=HB[:, 0:10], in0=pv("brg", 0, 10), scalar1=0.5, scalar2=None, op0=ALU.mult), reads=["PV"], writes=["HB"])
    P.op("dve", lambda e: e.tensor_scalar(out=HB[:, 10:20], in0=pv("big", 0, 10), scalar1=0.5, scalar2=None, op0=ALU.mult), reads=["PV", "HB"], writes=["HB"])
    P.op("act", lambda e: e.activation(out=CTs, in_=CTs, func=AF.Silu), reads=["CTs"], writes=["CTs"])
    CTb = alloc_late("CTb", [128, 8, 17], BF16)
    P.op("dve", lambda e: e.tensor_copy(out=CTb, in_=CTs), reads=["CTs"], writes=["CTb"])

    def adaln(fc):
        w, wk = wget(("w_ada", fc), 8)
        b = psum()
        if w is not None:
            for k in range(8):
                P.op("pe", lambda e, b=b, k=k, w=w: e.matmul(PSB[b][:, 0:17], lhsT=w[:, k, :], rhs=CTb[:, k, :], start=(k == 0), stop=(k == 7)),
                     reads=[wk, "CTb"], writes=[("ps", b)])
        P.op("act", lambda e, b=b, fc=fc: e.activation(out=MOD[:, fc, :], in_=PSB[b][:, 0:17], func=AF.Identity, bias=pv("bada", fc), scale=1.0),
             reads=[("ps", b), "PV"], writes=[("MOD", fc)])
        sec, ch = fc // 8, fc % 8
        if sec in (1, 4):
            gn = "g1" if sec == 1 else "g2"
            P.op("dve", lambda e: e.tensor_scalar(out=MOD[:, fc, :], in0=MOD[:, fc, :], scalar1=1.0, scalar2=pv(gn, ch), op0=ALU.add, op1=ALU.mult),
                 reads=[("MOD", fc), "PV"], writes=[("MOD", fc)])

    def load_x(p):
        for ch in range(8):
            P.dma("sp", X[:, ch, 0:TP], dram["xT"][:, ch, p * TP:(p + 1) * TP], writes=[("X", ch, 0), ("X", ch, 1)])
        if p == 0:
            P.dma("sp", X[:, :, TP:TP + NS * TS], dram["xT"][:, :, T:T + NS * TS], writes=[("X", ch, 2) for ch in range(8)])

    load_x(0)
    MODK = [("MOD", i) for i in range(48)]

    def affine(ct, out, in_, sec_scale, sec_shift, ch, reads, writes):
        if ct.s:
            P.op("dve", lambda e: e.tensor_tensor(out=in_, in0=in_, in1=modp(ct, sec_scale, ch), op=ALU.mult), reads=reads + MODK, writes=reads)
            P.op("dve", lambda e: e.tensor_tensor(out=out, in0=in_, in1=modp(ct, sec_shift, ch), op=ALU.add), reads=reads + MODK, writes=writes)
        else:
            P.op("dve", lambda e: e.tensor_scalar(out=out, in0=in_, scalar1=modp(ct, sec_scale, ch), scalar2=modp(ct, sec_shift, ch),
                                                  op0=ALU.mult, op1=ALU.add), reads=reads + MODK, writes=writes)

    def resid(ct, b, sec_gate, ch):
        xk = ("X", ch, ct.idx)
        xv = ct.flat(X, ch)
        if ct.s:
            tb, tk = nexttb()
            P.op("dve", lambda e: e.tensor_tensor(out=ct.tmp(tb), in0=psv(ct, b), in1=modp(ct, sec_gate, ch), op=ALU.mult),
                 reads=[("ps", b)] + MODK, writes=[tk])
            P.op("dve", lambda e: e.tensor_tensor(out=xv, in0=ct.tmp(tb), in1=xv, op=ALU.add), reads=[tk, xk], writes=[xk])
        else:
            P.op("dve", lambda e: e.scalar_tensor_tensor(out=xv, in0=psv(ct, b), scalar=modp(ct, sec_gate, ch), in1=xv, op0=ALU.mult, op1=ALU.add),
                 reads=[("ps", b), xk] + MODK, writes=[xk])

    def rms_rstd(ct, src=None, sname="X"):
        src = X if src is None else src
        b = psum()
        for ch in range(8):
            sq = SQ[:, ch % 2, 0:ct.cn]
            P.op("act", lambda e, ch=ch, sq=sq: e.activation(out=sq, in_=ct.flat2(src, ch), func=AF.Square), reads=[(sname, ch, ct.idx)], writes=[("SQ", ch % 2)])
            P.op("pe", lambda e, ch=ch, sq=sq: e.matmul(PSB[b][:, 0:ct.cn], lhsT=ONES, rhs=sq, start=(ch == 0), stop=(ch == 7)),
                 reads=[("SQ", ch % 2), "ONES"], writes=[("ps", b)])
        rs, rk = nextstat()
        P.op("act", lambda e: e.activation(out=rs[:, 0:ct.cn], in_=PSB[b][:, 0:ct.cn], func=AF.Ln, bias=EPSB, scale=1.0 / D),
             reads=[("ps", b), "EPSB"], writes=[rk])
        P.op("act", lambda e: e.activation(out=rs[:, 0:ct.cn], in_=rs[:, 0:ct.cn], func=AF.Exp, scale=-0.5), reads=[rk], writes=[rk])
        return rs, rk


    def norm_chunk(ct, sec_scale, sec_shift, ch, rsk, src=None, sname="X"):
        src = X if src is None else src
        rs, rk = rsk
        tb, tk = nexttb()
        P.op("dve", lambda e: e.tensor_tensor(out=ct.tmp(tb), in0=ct.flat(src, ch), in1=ct.tmp(rs), op=ALU.mult),
             reads=[(sname, ch, ct.idx), rk], writes=[tk])
        affine(ct, ct.flat(H, ch), ct.tmp(tb), sec_scale, sec_shift, ch, [tk], [("H", ch, ct.idx)])

    def norm_to_H(ct, sec_scale, sec_shift, pre=None, src=None, sname="X"):
        rsk = pre if pre is not None else rms_rstd(ct, src, sname)
        for ch in range(8):
            norm_chunk(ct, sec_scale, sec_shift, ch, rsk, src, sname)

    hr = lambda k, ct: ct.flat2(H, k)
    hk = lambda k, ct: [("H", k, ct.idx)]

    pre_rstd = {ct.idx: rms_rstd(ct) for ct in pass_tiles(0)}
    cast_mode[0] = "act"
    for k in range(8):
        adaln(8 + k)
        adaln(k)
        for ct in pass_tiles(0):
            norm_chunk(ct, 1, 0, k, pre_rstd[ct.idx])
    cast_mode[0] = "act"

    for p in range(NPASS):
        tiles = pass_tiles(p)
        last = p == NPASS - 1
        if p == 0:
            P.dma("pool", CS32, dram["cs_in"], writes=["CS32"])
            for ch in range(8):
                P.op("pool", lambda e, ch=ch: e.memset(UC[:, ch, 0:30], 0.0), writes=[("UC", ch, 0)])
                sv = UC[:, ch, 30 + TP:30 + TP + NS * 38].rearrange("p (s w) -> p s w", w=38)
                P.op("pool", lambda e, ch=ch, sv=sv: e.tensor_copy(out=sv[:, :, 0:30], in_=CS32[:, ch, :, :]), reads=["CS32"], writes=[("UC", ch, "s")])
                P.op("pool", lambda e, ch=ch: e.tensor_copy(out=CSO[:, ch, :, 0:22], in_=CS32[:, ch, :, 8:30]), reads=["CS32"], writes=[("CSO", ch)])
        else:
            for ch in range(8):
                P.op("pool", lambda e, ch=ch: e.tensor_copy(out=UC[:, ch, 0:30], in_=UH[:, ch, :]), reads=[("UH", ch)], writes=[("UC", ch, 0)])
            for ch in range(8):
                P.op("act", lambda e, ch=ch: e.activation(out=X[:, ch, 0:TP], in_=XN[:, ch, :], func=AF.Copy), reads=[("XN", ch, 0), ("XN", ch, 1)], writes=[("X", ch, 0), ("X", ch, 1)])
        def dg_build(ch):
            dsl = ch % 2
            P.op("dve", lambda e: e.tensor_tensor(out=DG[:, dsl, :, :], in0=IDB.unsqueeze(1).to_broadcast([128, KC, 128]),
                                                  in1=pv("wcd", ch * 31, 31).unsqueeze(2).to_broadcast([128, KC, 128]), op=ALU.mult),
                 reads=["IDB", "PV"], writes=[("DG", dsl)])

        dg_build(0)
        dg_build(1)
        for ch in range(8):
            ov = proj(("w_in", ch), 8, hr, hk, tiles)
            og = proj(("w_in", 8 + ch), 8, hr, hk, tiles)
            for (ct, bv), (_, bg) in zip(ov, og):
                tb, tk = nexttb()
                P.op("act", lambda e, ct=ct, bg=bg, tb=tb: e.activation(out=ct.tmp(tb), in_=psv(ct, bg), func=AF.Sigmoid), reads=[("ps", bg)], writes=[tk])
                P.op("dve", lambda e, ct=ct, bv=bv, ch=ch, tb=tb: e.tensor_tensor(out=ct.halo_w(UC, ch, 30), in0=psv(ct, bv), in1=ct.tmp(tb), op=ALU.mult),
                     reads=[("ps", bv), tk], writes=ct.k_w("UC", ch, 30))
            if p == 0:
                sv = UC[:, ch, 30 + TP:30 + TP + NS * 38].rearrange("p (s w) -> p s w", w=38)
                P.op("pool", lambda e, sv=sv, ch=ch: e.tensor_copy(out=CSO[:, ch, :, 22:30], in_=sv[:, :, 30:38]), reads=[("UC", ch, "s")], writes=[("CSO", ch)])
            if not last:
                P.op("pool", lambda e, ch=ch: e.tensor_copy(out=UH[:, ch, :], in_=UC[:, ch, TP:TP + 30]), reads=[("UC", ch, 2)], writes=[("UH", ch)])
            else:
                o, _ = STO["conf_p"]
                P.op("pool", lambda e, ch=ch, o=o: e.tensor_copy(out=ST[:, o + ch * 30:o + (ch + 1) * 30], in_=UC[:, ch, TP:TP + 30]),
                     reads=[("UC", ch, 2)], writes=[("ST", "conf_p", ch)])
        if p == 0:
            P.dma("pool", dram["cso"], CSO, reads=[("CSO", ch) for ch in range(8)])
        ada_pending = list(range(16, 48)) if p == 0 else []
        for ch in range(8):
            dsl = ch % 2
            if ch >= 2:
                dg_build(ch)
            for ct in tiles:
                b = psum()
                for j in range(KC):
                    P.op("pe", lambda e, ch=ch, j=j, dsl=dsl, ct=ct, b=b: e.matmul(PSB[b][:, 0:ct.cn], lhsT=DG[:, dsl, j, :], rhs=ct.halo_r(UC, ch, 30, j),
                                                                                  start=(j == 0), stop=(j == KC - 1)),
                         reads=[("DG", dsl)] + ct.k_r("UC", ch, 30), writes=[("ps", b)])
                P.op("act", lambda e, ch=ch, ct=ct, b=b: e.activation(out=ct.halo_r(UC, ch, 30, 0), in_=psv(ct, b), func=AF.Identity, bias=pv("bcd", ch), scale=1.0),
                     reads=[("ps", b), "PV"], writes=ct.k_s("UC", ch))
                for _ in range(2 if ct.idx == 0 else 1):
                    if ada_pending:
                        adaln(ada_pending.pop(0))
        while ada_pending:
            adaln(ada_pending.pop(0))
        def mga(f):
            om = proj(("w_in", 36 + f), 8, hr, hk, tiles)
            for ct, bm in om:
                P.op("act", lambda e, ct=ct, bm=bm: e.activation(out=ct.flat(YA, f), in_=psv(ct, bm), func=AF.Tanh, scale=0.5), reads=[("ps", bm)], writes=[("YA", f, ct.idx)])

        fsplit = [[0, 1, 2], [3, 4, 5], [6, 7]] if len(tiles) == 3 else [[0, 1, 2, 3], [4, 5, 6, 7]]
        for ti, ct in enumerate(tiles):
            bs, bq = psum(), psum()
            for ch in range(8):
                cv = ct.halo_r(UC, ch, 30, 0)
                sq = SQ[:, ch % 2, 0:ct.cn]
                P.op("dve", lambda e, cv=cv, sq=sq, ct=ct: e.tensor_tensor(out=ct.v(sq), in0=cv, in1=cv, op=ALU.mult), reads=ct.k_s("UC", ch), writes=[("SQ", ch % 2)])
                P.op("pe", lambda e, cv=cv, ch=ch, ct=ct, bs=bs: e.matmul(PSB[bs][:, 0:ct.cn], lhsT=ONES, rhs=cv, start=(ch == 0), stop=(ch == 7)),
                     reads=ct.k_s("UC", ch) + ["ONES"], writes=[("ps", bs)])
                P.op("pe", lambda e, sq=sq, ch=ch, ct=ct, bq=bq: e.matmul(PSB[bq][:, 0:ct.cn], lhsT=ONES, rhs=sq, start=(ch == 0), stop=(ch == 7)),
                     reads=[("SQ", ch % 2), "ONES"], writes=[("ps", bq)])
            cn = ct.cn
            mean, mk = nextstat()
            rstd, rk = nextstat()
            P.op("act", lambda e, bs=bs, cn=cn, mean=mean: e.activation(out=mean[:, 0:cn], in_=PSB[bs][:, 0:cn], func=AF.Identity, scale=1.0 / D), reads=[("ps", bs)], writes=[mk])
            P.op("dve", lambda e, cn=cn, mean=mean, rstd=rstd: e.tensor_tensor(out=rstd[:, 0:cn], in0=mean[:, 0:cn], in1=mean[:, 0:cn], op=ALU.mult), reads=[mk], writes=[rk])
            P.op("dve", lambda e, cn=cn, bq=bq, rstd=rstd: e.scalar_tensor_tensor(out=rstd[:, 0:cn], in0=PSB[bq][:, 0:cn], scalar=1.0 / D, in1=rstd[:, 0:cn], op0=ALU.mult, op1=ALU.subtract),
                 reads=[("ps", bq), rk], writes=[rk])
            P.op("act", lambda e, cn=cn, rstd=rstd: e.activation(out=rstd[:, 0:cn], in_=rstd[:, 0:cn], func=AF.Ln, bias=EPSB, scale=1.0), reads=[rk, "EPSB"], writes=[rk])
            P.op("act", lambda e, cn=cn, rstd=rstd: e.activation(out=rstd[:, 0:cn], in_=rstd[:, 0:cn], func=AF.Exp, scale=-0.5), reads=[rk], writes=[rk])
            for f in fsplit[ti]:
                mga(f)
            for ch in range(8):
                cv = ct.halo_r(UC, ch, 30, 0)
                tb, tk = nexttb()
                P.op("dve", lambda e, cv=cv, ct=ct, tb=tb, mean=mean: e.tensor_tensor(out=ct.tmp(tb), in0=cv, in1=ct.tmp(mean), op=ALU.subtract), reads=ct.k_s("UC", ch) + [mk], writes=[tk])
                P.op("dve", lambda e, ct=ct, tb=tb, rstd=rstd: e.tensor_tensor(out=ct.tmp(tb), in0=ct.tmp(tb), in1=ct.tmp(rstd), op=ALU.mult), reads=[tk, rk], writes=[tk])
                P.op("act", lambda e, cv=cv, ct=ct, ch=ch, tb=tb: e.activation(out=cv, in_=ct.tmp(tb), func=AF.Silu, bias=pv("bln", ch), scale=pv("gln", ch)),
                     reads=[tk, "PV"], writes=ct.k_s("UC", ch))
        ar = lambda k, ct: ct.halo_r(UC, k, 30, 0)
        ak = lambda k, ct: ct.k_s("UC", k)
        for f in range(8):
            oy = proj(("w_co", f), 8, ar, ak, tiles)
            for ct, by in oy:
                P.op("dve", lambda e, ct=ct, by=by, f=f: e.scalar_tensor_tensor(out=ct.flat(YA, f), in0=ct.flat(YA, f), scalar=1.0, in1=psv(ct, by), op0=ALU.add, op1=ALU.mult),
                     reads=[("ps", by), ("YA", f, ct.idx)], writes=[("YA", f, ct.idx)])
        P.fence()
        if p == 0:
            o, _ = STO["rconv_s"]
            for j in range(10):
                P.op("pool", lambda e, j=j: e.memset(RX[:, j, 0:3], 0.0), writes=[("RX", j, 0)])
                sv = RX[:, j, 3 + TP:3 + TP + NS * 11].rearrange("p (s w) -> p s w", w=11)
                iv = ST[:, o + j * 48:o + (j + 1) * 48].rearrange("p (s w) -> p s w", w=3)
                P.op("pool", lambda e, sv=sv, iv=iv: e.tensor_copy(out=sv[:, :, 0:3], in_=iv), reads=[("ST", "rconv_s")], writes=[("RX", j, "s")])
        else:
            for j in range(10):
                P.op("pool", lambda e, j=j: e.tensor_copy(out=RX[:, j, 0:3], in_=RXH[:, j, :]), reads=[("RXH", j)], writes=[("RX", j, 0)])
        def b1(j):
            ox = proj(("w_in", 16 + j), 8, hr, hk, tiles)
            for ct, b in ox:
                P.op("act", lambda e, ct=ct, b=b: e.activation(out=ct.halo_w(RX, j, 3), in_=psv(ct, b), func=AF.Copy), reads=[("ps", b)], writes=ct.k_w("RX", j, 3))
            if p == 0:
                o, _ = STO["rconv_s"]
                sv = RX[:, j, 3 + TP:3 + TP + NS * 11].rearrange("p (s w) -> p s w", w=11)
                dv = ST[:, o + j * 48:o + (j + 1) * 48].rearrange("p (s w) -> p s w", w=3)
                P.op("pool", lambda e: e.tensor_copy(out=dv, in_=sv[:, :, 8:11]), reads=[("RX", j, "s")], writes=[("ST", "rconv_s")])
            if not last:
                P.op("pool", lambda e: e.tensor_copy(out=RXH[:, j, :], in_=RX[:, j, TP:TP + 3]), reads=[("RX", j, 2)], writes=[("RXH", j)])
            else:
                o, _ = STO["rconv_p"]
                P.op("pool", lambda e: e.tensor_copy(out=ST[:, o + j * 3:o + (j + 1) * 3], in_=RX[:, j, TP:TP + 3]), reads=[("RX", j, 2)], writes=[("ST", "rconv_p")])

        def b2(j):
            for ct in tiles:
                tb, tk = nexttb()
                P.op("dve", lambda e, ct=ct, tb=tb: e.tensor_scalar(out=ct.tmp(tb), in0=ct.halo_r(RX, j, 3, 0), scalar1=pv("wrc", j * 4 + 0), scalar2=pv("brc", j),
                                                                  op0=ALU.mult, op1=ALU.add), reads=ct.k_r("RX", j, 3) + ["PV"], writes=[tk])
                for tap in (1, 2):
                    P.op("dve", lambda e, ct=ct, tap=tap, tb=tb: e.scalar_tensor_tensor(out=ct.tmp(tb), in0=ct.halo_r(RX, j, 3, tap), scalar=pv("wrc", j * 4 + tap),
                                                                                       in1=ct.tmp(tb), op0=ALU.mult, op1=ALU.add),
                         reads=ct.k_r("RX", j, 3) + ["PV", tk], writes=[tk])
                P.op("dve", lambda e, ct=ct, tb=tb: e.scalar_tensor_tensor(out=ct.halo_r(RX, j, 3, 0), in0=ct.halo_r(RX, j, 3, 3), scalar=pv("wrc", j * 4 + 3),
                                                                          in1=ct.tmp(tb), op0=ALU.mult, op1=ALU.add),
                     reads=ct.k_r("RX", j, 3) + ["PV", tk], writes=ct.k_s("RX", j))

        xr = lambda k, ct: ct.halo_r(RX, k, 3, 0)

        def bset(j):
            o = 4 * (j % 2)
            return BT[o:o + 4], ["BT%d" % (o + ii) for ii in range(4)]

        NCP = TP + (NS * TS if p == 0 else 0)
        stile = [ct for ct in tiles if ct.s]

        def b3_p1(j):
            ncp = NCP
            (TR, TI, TAA, TM), names = bset(j)
            kl = KLIST[j]
            wr, wrk = wget(("w_rg", j), 3)
            wi, wik = wget(("w_ig", j), 3)
            for (w, wk, dst, dn, hbo) in ((wr, wrk, TR, names[0], 0), (wi, wik, TI, names[1], 10)):
                for ct in tiles:
                    b = psum()
                    if w is not None:
                        for t, k in enumerate(kl):
                            P.op("pe", lambda e, b=b, t=t, k=k, ct=ct, w=w: e.matmul(PSB[b][:, 0:ct.cn], lhsT=w[:, t, :], rhs=xr(k, ct), start=(t == 0), stop=(t == len(kl) - 1)),
                                 reads=[wk] + ct.k_s("RX", k), writes=[("ps", b)])
                    dk = (dn, ct.idx)
                    P.op("act", lambda e, ct=ct, b=b, dst=dst, hbo=hbo: e.activation(out=ct.flat(dst), in_=psv(ct, b), func=AF.Tanh, bias=HB[:, hbo + j:hbo + j + 1], scale=0.5),
                         reads=[("ps", b), "HB"], writes=[dk])
            allk = lambda n: [(n, ct.idx) for ct in tiles]
            P.op("act", lambda e: e.activation(out=TAA[:, 0:ncp], in_=TR[:, 0:ncp], func=AF.Exp, scale=CNEG[:, j:j + 1], bias=CNEG[:, j:j + 1]),
                 reads=allk(names[0]) + ["CNEG"], writes=allk(names[2]))
            P.op("act", lambda e: e.activation(out=TM[:, 0:ncp], in_=TR[:, 0:ncp], func=AF.Exp, scale=CNEG[:, 10 + j:11 + j], bias=CNEG[:, 10 + j:11 + j]),
                 reads=allk(names[0]) + ["CNEG2"], writes=allk(names[3]))

        def b3_sqrt(j):
            ncp = NCP
            (TR, TI, TAA, TM), names = bset(j)
            kM = [(names[3], ct.idx) for ct in tiles]
            P.op("act", lambda e: e.activation(out=TM[:, 0:ncp], in_=TM[:, 0:ncp], func=AF.Sqrt, bias=0.25, scale=-0.25), reads=kM, writes=kM)

        def b3_p2(j):
            ncp = NCP
            (TR, TI, TAA, TM), names = bset(j)
            allk = lambda n: [(n, ct.idx) for ct in tiles]
            kR, kI, kA, kM = [allk(n) for n in names]
            P.op("dve", lambda e: e.scalar_tensor_tensor(out=TI[:, 0:ncp], in0=TI[:, 0:ncp], scalar=1.0, in1=TM[:, 0:ncp], op0=ALU.add, op1=ALU.mult), reads=kI + kM, writes=kI)
            P.op("dve", lambda e: e.tensor_tensor(out=TI[:, 0:TP], in0=TI[:, 0:TP], in1=RX[:, j, 0:TP], op=ALU.mult),
                 reads=kI + [("RX", j, 0), ("RX", j, 1)], writes=kI)
            for ct in stile:
                kr, ki, ka, km = [(n, ct.idx) for n in names]
                P.op("dve", lambda e, ct=ct: e.tensor_tensor(out=ct.flat(TI), in0=ct.flat(TI), in1=xr(j, ct), op=ALU.mult), reads=[ki] + ct.k_s("RX", j), writes=[ki])
                o, _ = STO["h_s"]
                h0 = ST[:, o + j * 16:o + (j + 1) * 16]
                a0 = ct.flat(TAA)[:, :, 0]
                b0 = ct.flat(TI)[:, :, 0]
                P.op("dve", lambda e, a0=a0, h0=h0: e.tensor_tensor(out=FX, in0=a0, in1=h0, op=ALU.mult), reads=[ka, ("ST", "h_s")], writes=["FX"])
                P.op("dve", lambda e, b0=b0: e.tensor_tensor(out=b0, in0=b0, in1=FX, op=ALU.add), reads=[ki, "FX"], writes=[ki])
                P.op("dve", lambda e, a0=a0: e.memset(a0, 0.0), reads=["FX"], writes=[ka])
            P.op("dve", lambda e: e.tensor_tensor_scan(out=TR[:, 0:TP], data0=TAA[:, 0:TP], data1=TI[:, 0:TP], initial=HCAR[:, j:j + 1], op0=ALU.mult, op1=ALU.add),
                 reads=kA + kI + [("HCAR", j)], writes=kR)
            P.op("dve", lambda e: e.tensor_copy(out=HCAR[:, j:j + 1], in_=TR[:, TP - 1:TP]), reads=kR, writes=[("HCAR", j)])
            for ct in stile:
                kr, ki, ka, km = [(n, ct.idx) for n in names]
                o, _ = STO["h_s"]
                h0 = ST[:, o + j * 16:o + (j + 1) * 16]
                P.op("dve", lambda e, ct=ct: e.tensor_tensor_scan(out=ct.flat2(TR), data0=ct.flat2(TAA), data1=ct.flat2(TI), initial=0.0, op0=ALU.mult, op1=ALU.add),
                     reads=[ka, ki], writes=[kr])
                P.op("dve", lambda e, ct=ct, h0=h0: e.tensor_copy(out=h0, in_=ct.flat(TR)[:, :, TS - 1]), reads=[kr], writes=[("ST", "h_s")])
            if last:
                o, _ = STO["h_p"]
                P.op("dve", lambda e, o=o: e.tensor_copy(out=ST[:, o + j:o + j + 1], in_=HCAR[:, j:j + 1]), reads=[("HCAR", j)], writes=[("ST", "h_p")])
            hgk = [("HG", j, ct.idx) for ct in tiles]
            P.op("dve", lambda e: e.tensor_tensor(out=HG[:, j, 0:ncp], in0=TR[:, 0:ncp], in1=HG[:, j, 0:ncp], op=ALU.mult), reads=kR + hgk, writes=hgk)

        def b1g(j):
            og = proj(("w_in", 26 + j), 8, hr, hk, tiles)
            for ct, b in og:
                P.op("act", lambda e, ct=ct, b=b: e.activation(out=ct.flat(HG, j), in_=psv(ct, b), func=AF.Gelu_apprx_tanh), reads=[("ps", b)], writes=[("HG", j, ct.idx)])

        b1n, b2n = [0], [0]

        bgn = [0]

        def ensure_b1(upto, gupto=None):
            gupto = upto if gupto is None else gupto
            while b1n[0] <= min(9, upto) or bgn[0] <= min(9, gupto):
                if b1n[0] <= min(9, upto):
                    b1(b1n[0])
                    b1n[0] += 1
                if bgn[0] <= min(9, gupto):
                    b1g(bgn[0])
                    bgn[0] += 1

        def ensure_b2(upto):
            while b2n[0] <= min(9, upto):
                b2(b2n[0])
                b2n[0] += 1

        cast_mode[0] = "act"
        ensure_b1(4, -1)
        ensure_b2(4)
        for m in range(5):
            j0, j1 = 2 * m, 2 * m + 1
            b3_p1(j0)
            b3_sqrt(j0)
            b3_p1(j1)
            b3_sqrt(j1)
            ensure_b1(j1 + 5, j1 + 3)
            b3_p2(j0)
            b3_p2(j1)
            ensure_b2(j1 + 5)
        cast_mode[0] = "act"
        gr = lambda k, ct: ct.flat2(HG, k)
        gk = lambda k, ct: [("HG", k, ct.idx)]
        for f in range(8):
            oy = proj_multi([(("w_ro", f, 0), 5, 0), (("w_ro", f, 1), 5, 5)], gr, gk, tiles)
            om = proj(("w_in", 44 + f), 8, hr, hk, tiles)
            for (ct, by), (_, bm) in zip(oy, om):
                tb, tk = nexttb()
                P.op("act", lambda e, ct=ct, bm=bm, tb=tb: e.activation(out=ct.tmp(tb), in_=psv(ct, bm), func=AF.Sigmoid), reads=[("ps", bm)], writes=[tk])
                P.op("dve", lambda e, ct=ct, by=by, tb=tb: e.tensor_tensor(out=ct.tmp(tb), in0=psv(ct, by), in1=ct.tmp(tb), op=ALU.mult), reads=[("ps", by), tk], writes=[tk])
                P.op("dve", lambda e, ct=ct, f=f, tb=tb: e.scalar_tensor_tensor(out=ct.flat(YA, f), in0=ct.flat(YA, f), scalar=0.5, in1=ct.tmp(tb), op0=ALU.mult, op1=ALU.add),
                     reads=[tk, ("YA", f, ct.idx)], writes=[("YA", f, ct.idx)])
        P.fence()
        yr = lambda k, ct: ct.flat2(YA, k)
        yk = lambda k, ct: [("YA", k, ct.idx)]
        for f in range(8):
            oo = proj(("w_out", f), 8, yr, yk, tiles)
            for ct, b in oo:
                resid(ct, b, 2, f)
        for ct in tiles:
            norm_to_H(ct, 4, 3)
        NG = 24 // GF
        pairn = [0]

        def ffn_chunk(q, sl, c):
            if p == 0:
                o, _ = STO["ffn_s"]
                P.op("pool", lambda e: e.memset(UPB[:, sl, 0:2], 0.0), writes=[("UP", sl, 0)])
                sv = UPB[:, sl, 2 + TP:2 + TP + NS * 10].rearrange("p (s w) -> p s w", w=10)
                iv = ST[:, o + c * 32:o + (c + 1) * 32].rearrange("p (s w) -> p s w", w=2)
                P.op("pool", lambda e: e.tensor_copy(out=sv[:, :, 0:2], in_=iv), reads=[("ST", "ffn_s")], writes=[("UP", sl, "s")])
            else:
                P.op("pool", lambda e: e.tensor_copy(out=UPB[:, sl, 0:2], in_=UPH[:, c, :]), reads=[("UPH", c)], writes=[("UP", sl, 0)])
            ou = proj(("w_up", c), 8, hr, hk, tiles)
            for ct, b in ou:
                P.op("act", lambda e, ct=ct, b=b: e.activation(out=ct.halo_w(UPB, sl, 2), in_=psv(ct, b), func=AF.Copy), reads=[("ps", b)], writes=ct.k_w("UP", sl, 2))
            if p == 0:
                o, _ = STO["ffn_s"]
                sv = UPB[:, sl, 2 + TP:2 + TP + NS * 10].rearrange("p (s w) -> p s w", w=10)
                dv = ST[:, o + c * 32:o + (c + 1) * 32].rearrange("p (s w) -> p s w", w=2)
                P.op("pool", lambda e: e.tensor_copy(out=dv, in_=sv[:, :, 8:10]), reads=[("UP", sl, "s")], writes=[("ST", "ffn_s")])
            if not last:
                P.op("pool", lambda e: e.tensor_copy(out=UPH[:, c, :], in_=UPB[:, sl, TP:TP + 2]), reads=[("UP", sl, 2)], writes=[("UPH", c)])
            else:
                o, _ = STO["ffn_p"]
                P.op("pool", lambda e: e.tensor_copy(out=ST[:, o + c * 2:o + (c + 1) * 2], in_=UPB[:, sl, TP:TP + 2]), reads=[("UP", sl, 2)], writes=[("ST", "ffn_p")])

        def ffn_pair_proj(q, i):
            ub = (q % 2) * 2 * GF
            ffn_chunk(q, ub + i, q * GF + i)
            ffn_chunk(q, ub + GF + i, 24 + q * GF + i)

        def ffn_pair(q, i):
            ub = (q % 2) * 2 * GF
            gb = (q % 2) * GF
            slg, slv = ub + i, ub + GF + i
            cg, cv = q * GF + i, 24 + q * GF + i
            dk = pairn[0] % 2
            for (c, o3) in ((cg, 0), (cv, 3)):
                P.op("dve", lambda e, c=c, o3=o3: e.tensor_tensor(out=DG3[:, dk, o3:o3 + 3, :], in0=IDB.unsqueeze(1).to_broadcast([128, 3, 128]),
                                                                in1=pv("wfd", c * 3, 3).unsqueeze(2).to_broadcast([128, 3, 128]), op=ALU.mult),
                     reads=["IDB", "PV"], writes=[("DG3", dk, o3)])
            for ct in tiles:
                k = pairn[0] % 4
                pairn[0] += 1
                fa, na = FT[k], "FT%d" % k
                bg, bv = psum(), psum()
                for (sl, o3, b) in ((slg, 0, bg), (slv, 3, bv)):
                    for tap in range(3):
                        P.op("pe", lambda e, ct=ct, sl=sl, o3=o3, b=b, tap=tap: e.matmul(PSB[b][:, 0:ct.cn], lhsT=DG3[:, dk, o3 + tap, :], rhs=ct.halo_r(UPB, sl, 2, tap),
                                                                                      start=(tap == 0), stop=(tap == 2)),
                             reads=[("DG3", dk, o3)] + ct.k_r("UP", sl, 2), writes=[("ps", b)])
                P.op("act", lambda e, ct=ct, fa=fa, bg=bg: e.activation(out=ct.tmp(fa), in_=psv(ct, bg), func=AF.Gelu_apprx_tanh, bias=pv("bfd", cg), scale=1.0),
                     reads=[("ps", bg), "PV"], writes=[na])
                P.op("dve", lambda e, ct=ct, fa=fa, bv=bv: e.scalar_tensor_tensor(out=ct.flat(GV, gb + i), in0=psv(ct, bv), scalar=pv("bfd", cv), in1=ct.tmp(fa),
                                                                                op0=ALU.add, op1=ALU.mult),
                     reads=[("ps", bv), na, "PV"], writes=[("GV", gb + i, ct.idx)])

        def ffn_down(q):
            gb = (q % 2) * GF
            vr = lambda k, ct: ct.flat2(GV, gb + k)
            vk = lambda k, ct: [("GV", gb + k, ct.idx)]
            for f in range(8):
                od = proj(("w_dn", q, f), GF, vr, vk, tiles)
                for ct, b in od:
                    resid(ct, b, 5, f)

        pairs = [(q, i) for q in range(NG) for i in range(GF)]
        pending = []
        for n in range(len(pairs) + 2):
            if n < len(pairs):
                ffn_pair_proj(*pairs[n])
            for (q, rdy) in list(pending):
                if rdy <= n:
                    ffn_down(q)
                    pending.remove((q, rdy))
            if 1 <= n <= len(pairs):
                q, i = pairs[n - 1]
                ffn_pair(q, i)
                if i == GF - 1:
                    pending.append((q, n + 1))
        assert not pending
        if not last:
            P.fence()
            for ch in range(8):
                P.dma("sp", XN[:, ch, :], dram["xT"][:, ch, (p + 1) * TP:(p + 2) * TP], writes=[("XN", ch, 0), ("XN", ch, 1)])
        else:
            stk = [("ST", "conf_p", ch) for ch in range(8)] + [("ST", n) for n in ("rconv_s", "rconv_p", "h_s", "h_p", "ffn_s", "ffn_p")]
            P.dma("pool", dram["st"], ST, reads=stk)
        for ct in tiles:
            rs, rk = rms_rstd(ct)
            for ch in range(8):
                xk = ("X", ch, ct.idx)
                P.op("dve", lambda e, ct=ct, ch=ch, rs=rs: e.scalar_tensor_tensor(out=ct.flat2(X, ch), in0=ct.flat2(X, ch), scalar=pv("gf", ch), in1=ct.tmp2(rs), op0=ALU.mult, op1=ALU.mult),
                     reads=[xk, rk, "PV"], writes=[xk])
            col0 = T if ct.s else p * TP + ct.c0
            P.dma("pool", dram["yT"][:, :, col0:col0 + ct.cn], X[:, :, ct.c0:ct.c0 + ct.cn], reads=[("X", ch, ct.idx) for ch in range(8)])
        if not last:
            for ct in pass_tiles(p + 1):
                norm_to_H(ct, 1, 0, src=XN, sname="XN")
    return wreq


def make_nc():
    nc = bass.Bass("TRN2", target_bir_lowering=False)
    dram = {}

    def din(name, shape):
        dram[name] = nc.dram_tensor(name, list(shape), F32, kind="ExternalInput").ap()

    din("xT", [128, 8, T + NS * TS])
    din("cT", [128, 8, 17])
    din("cs_in", [128, 8, NS, 30])
    din("rs_in", [128, 10 * NS * 3])
    din("h_in", [128, 10 * NS])
    din("fs_in", [128, 48 * NS * 2])
    din("pv", [128, NPV])
    din("idn", [128, 128])
    din("w_ada", [48, 128, 8 * 128])
    din("w_in", [52, 128, 8 * 128])
    din("w_co", [8, 128, 8 * 128])
    din("w_rg", [10, 128, 3 * 128])
    din("w_ig", [10, 128, 3 * 128])
    din("w_ro", [8, 2, 128, 5 * 128])
    din("w_out", [8, 128, 8 * 128])
    din("w_up", [48, 128, 8 * 128])
    din("w_dn", [24 // GF, 8, 128, GF * 128])
    dram["yT"] = nc.dram_tensor("yT", [128, 8, T + NS * TS], F32, kind="ExternalOutput").ap()
    dram["st"] = nc.dram_tensor("st", [128, NST], F32, kind="ExternalOutput").ap()
    dram["cso"] = nc.dram_tensor("cso", [128, 8, NS, 30], F32, kind="ExternalOutput").ap()
    return nc, dram


_CACHE = {}


def get_nc():
    if "nc" not in _CACHE:
        nc0, dram0 = make_nc()
        wreq = build_program(nc0, Prog(nc0, dry=True), None, dram0)
        nc, dram = make_nc()
        P = Prog(nc)
        build_program(nc, P, wreq, dram)
        P.emit()
        _CACHE["nc"] = nc
    return _CACHE["nc"]


def fm(a, nch):
    a = np.asarray(a)
    lead = a.shape[:-1]
    a = a.reshape(lead + (nch, 128))
    nd = a.ndim
    perm = (nd - 1, nd - 2) + tuple(range(nd - 2))
    return np.ascontiguousarray(a.transpose(perm))


def slab(w, kc):
    K, Fd = w.shape
    a = w.reshape(kc, 128, Fd // 128, 128).transpose(2, 1, 0, 3)
    return np.ascontiguousarray(a).reshape(Fd // 128, 128, kc * 128)


def kernel(x_prompt, x_sample, state_conf_conv, state_rnn_conv, state_rnn_h, state_ffn_conv,
           c_prompt, c_sample, w_ada, b_ada, g_norm1, w_in, w_conf_dw, b_conf_dw, g_conf_ln,
           b_conf_ln, w_conf_out, w_rnn_conv, b_rnn_conv, w_rg, b_rg, w_ig, b_ig, lru_lambda,
           w_rnn_out, w_out, g_norm2, w_up, w_ffn_dw, b_ffn_dw, w_down, g_final):
    f32 = np.float32
    A = lambda v: np.asarray(v, dtype=f32)
    x_prompt, x_sample = A(x_prompt), A(x_sample)
    pvv = np.zeros((128, NPV), f32)

    def put(name, arr):
        w = arr.reshape(128, -1).shape[1]
        pvv[:, PVO[name]:PVO[name] + w] = arr.reshape(128, -1)

    put("g1", fm(A(g_norm1)[0], 8)); put("g2", fm(A(g_norm2)[0], 8)); put("gf", fm(A(g_final), 8))
    put("bcd", fm(A(b_conf_dw)[0], 8)); put("gln", fm(A(g_conf_ln)[0], 8)); put("bln", fm(A(b_conf_ln)[0], 8))
    put("wcd", fm(A(w_conf_dw)[0], 8).transpose(0, 1, 2))
    put("brc", fm(A(b_rnn_conv)[0], 10)); put("wrc", fm(A(w_rnn_conv)[0], 10))
    put("brg", fm(A(b_rg)[0], 10)); put("big", fm(A(b_ig)[0], 10)); put("lam", fm(A(lru_lambda)[0], 10))
    put("bfd", fm(A(b_ffn_dw)[0], 48))
    put("wfd", fm(A(w_ffn_dw)[0], 48))
    put("bada", fm(A(b_ada)[0], 48))
    idn = np.eye(128, dtype=f32)
    W_ada = slab(A(w_ada)[0], 8)
    W_in = slab(A(w_in)[0], 8)
    W_co = slab(A(w_conf_out)[0], 8)
    W_ro = np.ascontiguousarray(slab(A(w_rnn_out)[0], 10).reshape(8, 128, 2, 640).transpose(0, 2, 1, 3))
    W_out = slab(A(w_out)[0], 8)
    W_up = slab(A(w_up)[0], 8)
    wd = A(w_down)[0]
    NG = 24 // GF
    W_dn = np.ascontiguousarray(wd.reshape(NG, GF, 128, 8, 128).transpose(0, 3, 2, 1, 4)).reshape(NG, 8, 128, GF * 128)

    def gate_slab(wg):
        full = np.zeros((DR, DR), f32)
        for g in range(8):
            full[g * 160:(g + 1) * 160, g * 160:(g + 1) * 160] = wg[g]
        out = np.zeros((10, 128, 3, 128), f32)
        for j in range(10):
            for t, k in enumerate(KLIST[j]):
                out[j, :, t, :] = full[k * 128:(k + 1) * 128, j * 128:(j + 1) * 128]
        return out.reshape(10, 128, 3 * 128)

    W_rg = gate_slab(A(w_rg)[0])
    W_ig = gate_slab(A(w_ig)[0])
    shared = {"pv": pvv, "idn": idn, "w_ada": W_ada, "w_in": W_in, "w_co": W_co, "w_rg": W_rg, "w_ig": W_ig,
              "w_ro": W_ro, "w_out": W_out, "w_up": W_up, "w_dn": W_dn}
    in_maps = []
    scc, src, srh, sfc = A(state_conf_conv)[0], A(state_rnn_conv)[0], A(state_rnn_h)[0], A(state_ffn_conv)[0]
    cp, cs = A(c_prompt), A(c_sample)
    for i in range(NCORES):
        sl = slice(NS * i, NS * (i + 1))
        xp = fm(x_prompt[i], 8)
        xs = fm(x_sample[sl].reshape(NS * TS, D), 8)
        xT = np.ascontiguousarray(np.concatenate([xp, xs], axis=2))
        cT = np.ascontiguousarray(np.concatenate([fm(cs[sl], 8), fm(cp[i:i + 1], 8)], axis=2))
        m = dict(shared)
        m["xT"] = xT
        m["cT"] = cT
        m["cs_in"] = fm(scc[sl], 8)
        m["rs_in"] = fm(src[sl], 10).reshape(128, -1)
        m["h_in"] = fm(srh[sl], 10).reshape(128, -1)
        m["fs_in"] = fm(sfc[sl], 48).reshape(128, -1)
        in_maps.append(m)
    nc = get_nc()
    res = run_bass_kernel_spmd(nc, in_maps, core_ids=list(range(NCORES)))
    y_p = np.zeros((8, T, D), f32)
    y_s = np.zeros((128, TS, D), f32)
    conf_p = np.zeros((1, 8, 30, D), f32); rconv_p = np.zeros((1, 8, 3, DR), f32)
    h_p = np.zeros((1, 8, DR), f32); ffn_p = np.zeros((1, 8, 2, 2 * DFF), f32)
    conf_s = np.zeros((1, 128, 30, D), f32); rconv_s = np.zeros((1, 128, 3, DR), f32)
    h_s = np.zeros((1, 128, DR), f32); ffn_s = np.zeros((1, 128, 2, 2 * DFF), f32)

    def unfm(a):
        nd = a.ndim
        perm = tuple(range(2, nd)) + (1, 0)
        b = a.transpose(perm)
        return b.reshape(b.shape[:-2] + (b.shape[-2] * 128,))

    for i in range(NCORES):
        r = res.results[i]
        yT = np.asarray(r["yT"]).reshape(128, 8, T + NS * TS)
        y_p[i] = unfm(yT[:, :, 0:T])
        y_s[NS * i:NS * (i + 1)] = unfm(yT[:, :, T:]).reshape(NS, TS, D)
        stv = np.asarray(r["st"]).reshape(128, NST)
        sec = lambda n, shp: stv[:, STO[n][0]:STO[n][0] + STO[n][1]].reshape((128,) + shp)
        sl = slice(NS * i, NS * (i + 1))
        conf_p[0, i] = unfm(sec("conf_p", (8, 30)))
        rconv_p[0, i] = unfm(sec("rconv_p", (10, 3)))
        h_p[0, i] = unfm(sec("h_p", (10,)))
        ffn_p[0, i] = unfm(sec("ffn_p", (48, 2)))
        conf_s[0, sl] = unfm(np.asarray(r["cso"]).reshape(128, 8, NS, 30))
        rconv_s[0, sl] = unfm(sec("rconv_s", (10, NS, 3)))
        h_s[0, sl] = unfm(sec("h_s", (10, NS)))
        ffn_s[0, sl] = unfm(sec("ffn_s", (48, NS, 2)))
    return (y_p, y_s, conf_p, rconv_p, h_p, ffn_p, conf_s, rconv_s, h_s, ffn_s)
```

```python
import contextlib
import numpy as np
import concourse.bass as bass
import concourse.mybir as mybir
from concourse.bass_utils import run_bass_kernel_spmd

F32 = mybir.dt.float32
BF16 = mybir.dt.bfloat16
AF = mybir.ActivationFunctionType
ALU = mybir.AluOpType

NCORES = 8
D = 1024
T = 2048
NS = 16
TS = 8
DR = 1280
DFF = 3072
NPASS = 2
TP = T // NPASS
NCMAX = TP + NS * TS
EPS = 1e-6
KC = 31
GF = 4

COMPUTE = ("pe", "act", "dve", "pool")

PVO = {}
_o = 0
for _n, _w in [("g1", 8), ("g2", 8), ("gf", 8), ("bcd", 8), ("gln", 8), ("bln", 8), ("wcd", 8 * 31),
               ("brc", 10), ("wrc", 40), ("brg", 10), ("big", 10), ("lam", 10), ("bfd", 48), ("wfd", 144),
               ("bada", 48)]:
    PVO[_n] = _o
    _o += _w
NPV = _o

STO = {}
_o = 0
for _n, _w in [("conf_p", 8 * 30), ("rconv_p", 10 * 3), ("h_p", 10), ("ffn_p", 48 * 2),
               ("rconv_s", 10 * 16 * 3), ("h_s", 10 * 16), ("ffn_s", 48 * 16 * 2)]:
    STO[_n] = (_o, _w)
    _o += _w
NST = _o


def gate_klist():
    out = []
    for j in range(10):
        lo, hi = 128 * j, 128 * j + 127
        g_lo, g_hi = lo // 160, hi // 160
        k_lo, k_hi = (160 * g_lo) // 128, (160 * g_hi + 159) // 128
        out.append(list(range(k_lo, k_hi + 1)))
    return out


KLIST = gate_klist()


class Op:
    __slots__ = ("eng", "fn", "raw", "oth", "kind", "sig", "seq", "dsem", "dval", "deps")

    def __init__(self, eng, fn, raw, oth, kind):
        self.eng, self.fn, self.raw, self.oth, self.kind = eng, fn, raw, oth, kind
        self.sig = False
        self.seq = 0
        self.dsem = None
        self.dval = 0
        self.deps = ()


class Prog:
    def __init__(self, nc, n_dma_sems=10, dry=False):
        self.nc = nc
        self.ops = []
        self.lastw = {}
        self.readers = {}
        self.fence_deps = set()
        self.last_fence_at = -1
        self.n_dma_sems = n_dma_sems
        self.dry = dry

    def op(self, eng, fn, reads=(), writes=(), kind="c"):
        if self.dry:
            return -1
        raw = set()
        oth = set()
        for k in reads:
            w = self.lastw.get(k)
            if w is not None:
                raw.add(w)
        for k in writes:
            w = self.lastw.get(k)
            if w is not None:
                oth.add(w)
            oth.update(self.readers.get(k, ()))
        raw.update(self.fence_deps)
        i = len(self.ops)
        self.ops.append(Op(eng, fn, raw, oth, kind))
        for k in reads:
            self.readers.setdefault(k, []).append(i)
        for k in writes:
            self.lastw[k] = i
            self.readers[k] = []
        return i

    def dma(self, eng, out, in_, reads=(), writes=()):
        return self.op(eng, lambda e: e.dma_start(out=out, in_=in_), reads, writes, kind="dma")

    def fence(self):
        if self.dry:
            return
        deps = set()
        last = {}
        for i, o in enumerate(self.ops):
            if o.kind == "dma":
                if i > self.last_fence_at:
                    deps.add(i)
            else:
                last[o.eng] = i
        deps.update(last.values())
        self.last_fence_at = len(self.ops) - 1
        self.fence_deps = deps

    def emit(self):
        nc = self.nc
        ops = self.ops
        for o in ops:
            deps = set()
            for d in o.raw:
                p = ops[d]
                if p.kind != "dma" and o.kind != "dma" and p.eng == o.eng and o.eng == "pe":
                    continue
                deps.add(d)
            for d in o.oth:
                p = ops[d]
                if p.kind != "dma" and o.kind != "dma" and p.eng == o.eng:
                    continue
                deps.add(d)
            o.deps = deps
            for d in deps:
                ops[d].sig = True
        engs = {"pe": nc.tensor, "act": nc.scalar, "dve": nc.vector, "pool": nc.gpsimd, "sp": nc.sync}
        with contextlib.ExitStack() as st:
            sems = {e: st.enter_context(nc.semaphore("s_" + e)) for e in COMPUTE}
            dsems = {q: [st.enter_context(nc.semaphore("d_%s%d" % (q, j))) for j in range(self.n_dma_sems)]
                     for q in ("sp", "pool")}
            cnt = {e: 0 for e in COMPUTE}
            dcnt = {q: 0 for q in dsems}
            for o in ops:
                if o.kind == "dma":
                    n = dcnt[o.eng]
                    dcnt[o.eng] = n + 1
                    o.dsem = dsems[o.eng][n % self.n_dma_sems]
                    o.dval = 16 * (n // self.n_dma_sems + 1)
                elif o.sig:
                    cnt[o.eng] += 1
                    o.seq = cnt[o.eng]
            per_eng = {e: [] for e in engs}
            for o in ops:
                per_eng[o.eng].append(o)

            def ticket(o):
                if o.kind == "dma":
                    return o.dsem, o.dval
                return sems[o.eng], o.seq

            def run(ename, eobj):
                waited = {}
                for o in per_eng[ename]:
                    ws = {}
                    for d in o.deps:
                        s, v = ticket(ops[d])
                        if ws.get(s, 0) < v:
                            ws[s] = v
                    if o.kind == "dma" and o.dval > 16:
                        if ws.get(o.dsem, 0) < o.dval - 16:
                            ws[o.dsem] = o.dval - 16
                    for s, v in ws.items():
                        if waited.get(s, 0) >= v:
                            continue
                        eobj.wait_ge(s, v)
                        waited[s] = v
                    ins = o.fn(eobj)
                    if o.kind == "dma":
                        ins.then_inc(o.dsem, 16)
                    elif o.sig:
                        ins.then_inc(sems[o.eng], 1)
                if ename in dsems:
                    last = {}
                    for o in per_eng[ename]:
                        if o.kind == "dma":
                            last[o.dsem] = o.dval
                    for s, v in last.items():
                        if waited.get(s, 0) < v:
                            eobj.wait_ge(s, v)

            with nc.Block() as block:
                @block.tensor
                def _(e):
                    run("pe", e)

                @block.scalar
                def _(e):
                    run("act", e)

                @block.vector
                def _(e):
                    run("dve", e)

                @block.gpsimd
                def _(e):
                    run("pool", e)

                @block.sync
                def _(e):
                    run("sp", e)


class CT:
    def __init__(self, idx, c0, cn, kind):
        self.idx, self.c0, self.cn, self.kind = idx, c0, cn, kind
        self.s = kind == "s"

    def v(self, ap2):
        if self.s:
            return ap2.rearrange("p (s t) -> p s t", t=TS)
        return ap2

    def flat(self, buf, ch=None):
        a = buf[:, self.c0:self.c0 + self.cn] if ch is None else buf[:, ch, self.c0:self.c0 + self.cn]
        return self.v(a)

    def flat2(self, buf, ch=None):
        return buf[:, self.c0:self.c0 + self.cn] if ch is None else buf[:, ch, self.c0:self.c0 + self.cn]

    def tmp(self, t):
        return self.v(t[:, 0:self.cn])

    def tmp2(self, t):
        return t[:, 0:self.cn]

    def _sv(self, buf, ch, h):
        pb = h + TP
        return buf[:, ch, pb:pb + NS * (h + TS)].rearrange("p (s w) -> p s w", w=h + TS)

    def halo_w(self, buf, ch, h):
        if self.s:
            return self._sv(buf, ch, h)[:, :, h:h + TS]
        return buf[:, ch, h + self.c0:h + self.c0 + self.cn]

    def halo_r(self, buf, ch, h, j):
        if self.s:
            return self._sv(buf, ch, h)[:, :, j:j + TS]
        return buf[:, ch, self.c0 + j:self.c0 + j + self.cn]

    def k_w(self, name, ch, h):
        if self.s:
            return [(name, ch, "s")]
        a, b = (h + self.c0) // 512, (h + self.c0 + self.cn - 1) // 512
        return [(name, ch, q) for q in range(a, b + 1)]

    def k_r(self, name, ch, h):
        if self.s:
            return [(name, ch, "s")]
        a, b = self.c0 // 512, (self.c0 + self.cn + h - 1) // 512
        return [(name, ch, q) for q in range(a, b + 1)]

    def k_s(self, name, ch):
        if self.s:
            return [(name, ch, "s")]
        return [(name, ch, self.c0 // 512)]


def pass_tiles(p):
    tiles = [CT(0, 0, 512, "p"), CT(1, 512, 512, "p")]
    if p == 0:
        tiles.append(CT(2, TP, NS * TS, "s"))
    return tiles


def build_program(nc, P, wplan, dram):
    A = {}
    off = [16640]

    def alloc(name, shape, dt):
        nb = int(np.prod(shape[1:])) * (4 if dt == F32 else 2)
        nb = (nb + 63) // 64 * 64
        t = nc.alloc_sbuf_tensor_at(name, list(shape), dt, offset=off[0])
        off[0] += nb
        assert off[0] <= 229376, (name, off[0])
        A[name] = t.ap()
        return A[name]

    PV = alloc("PV", [128, NPV], F32)
    CNEG = alloc("CNEG", [128, 20], F32)
    HB = alloc("HB", [128, 20], F32)
    scr0 = off[0]
    IDF = alloc("IDF", [128, 128], F32)
    CTs = alloc("CTs", [128, 8, 17], F32)
    scr1 = off[0]
    off[0] = scr0
    DG4 = alloc("DG4", [128, 4, 128], BF16)
    assert off[0] <= scr1
    off[0] = scr1
    IDB = alloc("IDB", [128, 128], BF16)
    ONES = alloc("ONES", [128, 128], BF16)
    CTb_ = alloc("CTb", [128, 8, 17], BF16)
    alloc_late = lambda n, sh, dt: CTb_
    MOD = alloc("MOD", [128, 48, 17], F32)
    X = alloc("X", [128, 8, NCMAX], F32)
    H = alloc("H", [128, 8, NCMAX], BF16)
    YA = alloc("YA", [128, 8, NCMAX], BF16)
    ST = alloc("ST", [128, NST], F32)
    UH = alloc("UH", [128, 8, 30], BF16)
    RXH = alloc("RXH", [128, 10, 3], BF16)
    UPH = alloc("UPH", [128, 48, 2], BF16)
    HCAR = alloc("HCAR", [128, 10], F32)
    NWS, NWB = 3, 4
    WS = alloc("WS", [128, NWS, 8 * 128], F32)
    WB = alloc("WB", [128, NWB, 8 * 128], BF16)
    STT_ = [alloc("STAT%d" % i, [128, 512], F32) for i in range(3)]
    GT_ = [alloc("GT%d" % i, [128, 512], F32) for i in range(3)]
    FX = alloc("FX", [128, 16], F32)
    tbn = [0]
    stn = [0]

    def nexttb():
        tbn[0] += 1
        i = tbn[0] % 3
        return GT_[i], "GT%d" % i

    def nextstat():
        stn[0] += 1
        i = stn[0] % 3
        return STT_[i], "STAT%d" % i
    SQ = alloc("SQ", [128, 2, 512], BF16)
    EPSB = alloc("EPSB", [128, 1], F32)
    region = off[0]
    UC = alloc("UC", [128, 8, 30 + TP + NS * 38], BF16)
    DG = alloc("DG", [128, 2, KC, 128], BF16)
    CS32 = alloc("CS32", [128, 8, NS, 30], F32)
    CSO = alloc("CSO", [128, 8, NS, 30], F32)
    endA = off[0]
    off[0] = region
    RX = alloc("RX", [128, 10, 3 + TP + NS * 11], BF16)
    HG = alloc("HG", [128, 10, NCMAX], BF16)
    BT = [alloc("BT%d" % i, [128, NCMAX], F32) for i in range(8)]
    endB = off[0]
    off[0] = region
    UPB = alloc("UPB", [128, 4 * GF, 2 + TP + NS * 10], BF16)
    GV = alloc("GV", [128, 2 * GF, NCMAX], BF16)
    FT = [alloc("FT%d" % i, [128, 512], F32) for i in range(4)]
    DG3 = alloc("DG3", [128, 2, 6, 128], BF16)
    endF = off[0]
    off[0] = max(endA, endB, endF) - 8 * TP * 4
    assert off[0] >= region + 42496
    XN = alloc("XN", [128, 8, TP], F32)

    PSB = [nc.alloc_psum_tensor("psb%d" % i, [128, 512], F32).ap() for i in range(8)]
    psn = [0]

    def psum():
        b = psn[0] % 8
        psn[0] += 1
        return b

    def pv(name, i, w=1):
        o = PVO[name] + i
        return PV[:, o:o + w]

    def st(name):
        o, w = STO[name]
        return ST[:, o:o + w]

    cast_mode = ["act"]
    wreq = []
    wstate = {"emitted": 0}
    PF = 3

    def wload(m):
        key, kc, cast = wplan[m]
        src = dram[key[0]][key[1]] if len(key) == 2 else dram[key[0]][key[1], key[2]]
        ss = m % NWS
        P.dma("sp", WS[:, ss, 0:kc * 128], src, writes=[("ws", ss)])
        if cast:
            sb = m % NWB
            if (m % 2 == 0 and cast_mode[0] != "pool") or cast_mode[0] == "act":
                P.op("act", lambda e, ss=ss, sb=sb, kc=kc: e.activation(out=WB[:, sb, 0:kc * 128], in_=WS[:, ss, 0:kc * 128], func=AF.Copy),
                     reads=[("ws", ss)], writes=[("wb", sb)])
            else:
                P.op("pool", lambda e, ss=ss, sb=sb, kc=kc: e.tensor_copy(out=WB[:, sb, 0:kc * 128], in_=WS[:, ss, 0:kc * 128]),
                     reads=[("ws", ss)], writes=[("wb", sb)])

    def wget(key, kc, cast=True):
        if wplan is None:
            wreq.append((key, kc, cast))
            return None, None
        m = wstate.setdefault("next", 0)
        wstate["next"] = m + 1
        assert wplan[m] == (key, kc, cast), (m, wplan[m], key)
        while wstate["emitted"] < min(len(wplan), m + PF):
            wload(wstate["emitted"])
            wstate["emitted"] += 1
        if cast:
            sb = m % NWB
            return WB[:, sb, 0:kc * 128].rearrange("p (k f) -> p k f", f=128), ("wb", sb)
        ss = m % NWS
        return WS[:, ss, 0:kc * 128].rearrange("p (k f) -> p k f", f=128), ("ws", ss)

    def proj(key, kc, rhs_fn, rhs_keys_fn, tiles):
        w, wk = wget(key, kc)
        outs = []
        for ct in tiles:
            b = psum()
            if w is not None:
                for k in range(kc):
                    P.op("pe", lambda e, b=b, k=k, ct=ct, w=w: e.matmul(PSB[b][:, 0:ct.cn], lhsT=w[:, k, :], rhs=rhs_fn(k, ct),
                                                                      start=(k == 0), stop=(k == kc - 1)),
                         reads=[wk] + rhs_keys_fn(k, ct), writes=[("ps", b)])
            outs.append((ct, b))
        return outs

    def proj_multi(slabs, rhs_fn, rhs_keys_fn, tiles):
        banks = [psum() for _ in tiles]
        nk = sum(sl[1] for sl in slabs)
        done = 0
        for (key, kc, koff) in slabs:
            w, wk = wget(key, kc)
            if w is not None:
                for ct, b in zip(tiles, banks):
                    for k in range(kc):
                        P.op("pe", lambda e, b=b, k=k, ct=ct, w=w, koff=koff, first=(done + k == 0), lastk=(done + k == nk - 1):
                             e.matmul(PSB[b][:, 0:ct.cn], lhsT=w[:, k, :], rhs=rhs_fn(koff + k, ct), start=first, stop=lastk),
                             reads=[wk] + rhs_keys_fn(koff + k, ct), writes=[("ps", b)])
            done += kc
        return list(zip(tiles, banks))

    def psv(ct, b):
        return ct.v(PSB[b][:, 0:ct.cn])

    def modp(ct, sec, ch):
        if ct.s:
            return MOD[:, sec * 8 + ch, 0:16].unsqueeze(2).to_broadcast([128, NS, TS])
        return MOD[:, sec * 8 + ch, 16:17]

    P.dma("pool", PV, dram["pv"], writes=["PV"])
    P.dma("pool", IDF, dram["idn"], writes=["IDF"])
    P.dma("pool", CTs, dram["cT"], writes=["CTs"])
    P.dma("pool", st("ffn_s"), dram["fs_in"], writes=[("ST", "ffn_s")])
    P.dma("pool", st("rconv_s"), dram["rs_in"], writes=[("ST", "rconv_s")])
    P.dma("pool", st("h_s"), dram["h_in"], writes=[("ST", "h_s")])
    P.op("dve", lambda e: e.tensor_copy(out=IDB, in_=IDF), reads=["IDF"], writes=["IDB"])
    P.op("pool", lambda e: e.memset(ONES, 1.0), writes=["ONES"])
    P.op("pool", lambda e: e.memset(HCAR, 0.0), writes=[("HCAR", j) for j in range(10)])
    P.op("pool", lambda e: e.memset(EPSB, EPS), writes=["EPSB"])
    P.op("act", lambda e: e.activation(out=CNEG[:, 0:10], in_=pv("lam", 0, 10), func=AF.Exp, scale=-1.0), reads=["PV"], writes=["CNEG"])
    P.op("act", lambda e: e.activation(out=CNEG[:, 0:10], in_=CNEG[:, 0:10], func=AF.Ln, bias=1.0), reads=["CNEG"], writes=["CNEG"])
    P.op("dve", lambda e: e.tensor_scalar(out=CNEG[:, 10:20], in0=CNEG[:, 0:10], scalar1=-8.0, scalar2=None, op0=ALU.mult), reads=["CNEG"], writes=["CNEG2"])
    P.op("dve", lambda e: e.tensor_scalar(out=CNEG[:, 0:10], in0=CNEG[:, 0:10], scalar1=-4.0, scalar2=None, op0=ALU.mult), reads=["CNEG", "CNEG2"], writes=["CNEG"])
    P.op("dve", lambda e: e.tensor_scalar(out=HB[:, 0:10], in0=pv("brg", 0, 10), scalar1=0.5, scalar2=None, op0=ALU.mult), reads=["PV"], writes=["HB"])
    P.op("dve", lambda e: e.tensor_scalar(out=HB[:, 10:20], in0=pv("big", 0, 10), scalar1=0.5, scalar2=None, op0=ALU.mult), reads=["PV", "HB"], writes=["HB"])
    P.op("act", lambda e: e.activation(out=CTs, in_=CTs, func=AF.Silu), reads=["CTs"], writes=["CTs"])
    CTb = alloc_late("CTb", [128, 8, 17], BF16)
    P.op("dve", lambda e: e.tensor_copy(out=CTb, in_=CTs), reads=["CTs"], writes=["CTb"])

    def adaln(fc):
        w, wk = wget(("w_ada", fc), 8)
        b = psum()
        if w is not None:
            for k in range(8):
                P.op("pe", lambda e, b=b, k=k, w=w: e.matmul(PSB[b][:, 0:17], lhsT=w[:, k, :], rhs=CTb[:, k, :], start=(k == 0), stop=(k == 7)),
                     reads=[wk, "CTb"], writes=[("ps", b)])
        P.op("act", lambda e, b=b, fc=fc: e.activation(out=MOD[:, fc, :], in_=PSB[b][:, 0:17], func=AF.Identity, bias=pv("bada", fc), scale=1.0),
             reads=[("ps", b), "PV"], writes=[("MOD", fc)])
        sec, ch = fc // 8, fc % 8
        if sec in (1, 4):
            gn = "g1" if sec == 1 else "g2"
            P.op("dve", lambda e: e.tensor_scalar(out=MOD[:, fc, :], in0=MOD[:, fc, :], scalar1=1.0, scalar2=pv(gn, ch), op0=ALU.add, op1=ALU.mult),
                 reads=[("MOD", fc), "PV"], writes=[("MOD", fc)])

    def load_x(p):
        for ch in range(8):
            P.dma("sp", X[:, ch, 0:TP], dram["xT"][:, ch, p * TP:(p + 1) * TP], writes=[("X", ch, 0), ("X", ch, 1)])
        if p == 0:
            P.dma("sp", X[:, :, TP:TP + NS * TS], dram["xT"][:, :, T:T + NS * TS], writes=[("X", ch, 2) for ch in range(8)])

    load_x(0)
    MODK = [("MOD", i) for i in range(48)]

    def affine(ct, out, in_, sec_scale, sec_shift, ch, reads, writes):
        if ct.s:
            P.op("dve", lambda e: e.tensor_tensor(out=in_, in0=in_, in1=modp(ct, sec_scale, ch), op=ALU.mult), reads=reads + MODK, writes=reads)
            P.op("dve", lambda e: e.tensor_tensor(out=out, in0=in_, in1=modp(ct, sec_shift, ch), op=ALU.add), reads=reads + MODK, writes=writes)
        else:
            P.op("dve", lambda e: e.tensor_scalar(out=out, in0=in_, scalar1=modp(ct, sec_scale, ch), scalar2=modp(ct, sec_shift, ch),
                                                  op0=ALU.mult, op1=ALU.add), reads=reads + MODK, writes=writes)

    def resid(ct, b, sec_gate, ch):
        xk = ("X", ch, ct.idx)
        xv = ct.flat(X, ch)
        if ct.s:
            tb, tk = nexttb()
            P.op("dve", lambda e: e.tensor_tensor(out=ct.tmp(tb), in0=psv(ct, b), in1=modp(ct, sec_gate, ch), op=ALU.mult),
                 reads=[("ps", b)] + MODK, writes=[tk])
            P.op("dve", lambda e: e.tensor_tensor(out=xv, in0=ct.tmp(tb), in1=xv, op=ALU.add), reads=[tk, xk], writes=[xk])
        else:
            P.op("dve", lambda e: e.scalar_tensor_tensor(out=xv, in0=psv(ct, b), scalar=modp(ct, sec_gate, ch), in1=xv, op0=ALU.mult, op1=ALU.add),
                 reads=[("ps", b), xk] + MODK, writes=[xk])

    def rms_rstd(ct, src=None, sname="X"):
        src = X if src is None else src
        b = psum()
        for ch in range(8):
            sq = SQ[:, ch % 2, 0:ct.cn]
            P.op("act", lambda e, ch=ch, sq=sq: e.activation(out=sq, in_=ct.flat2(src, ch), func=AF.Square), reads=[(sname, ch, ct.idx)], writes=[("SQ", ch % 2)])
            P.op("pe", lambda e, ch=ch, sq=sq: e.matmul(PSB[b][:, 0:ct.cn], lhsT=ONES, rhs=sq, start=(ch == 0), stop=(ch == 7)),
                 reads=[("SQ", ch % 2), "ONES"], writes=[("ps", b)])
        rs, rk = nextstat()
        P.op("act", lambda e: e.activation(out=rs[:, 0:ct.cn], in_=PSB[b][:, 0:ct.cn], func=AF.Ln, bias=EPSB, scale=1.0 / D),
             reads=[("ps", b), "EPSB"], writes=[rk])
        P.op("act", lambda e: e.activation(out=rs[:, 0:ct.cn], in_=rs[:, 0:ct.cn], func=AF.Exp, scale=-0.5), reads=[rk], writes=[rk])
        return rs, rk


    def norm_chunk(ct, sec_scale, sec_shift, ch, rsk, src=None, sname="X"):
        src = X if src is None else src
        rs, rk = rsk
        tb, tk = nexttb()
        P.op("dve", lambda e: e.tensor_tensor(out=ct.tmp(tb), in0=ct.flat(src, ch), in1=ct.tmp(rs), op=ALU.mult),
             reads=[(sname, ch, ct.idx), rk], writes=[tk])
        affine(ct, ct.flat(H, ch), ct.tmp(tb), sec_scale, sec_shift, ch, [tk], [("H", ch, ct.idx)])

    def norm_to_H(ct, sec_scale, sec_shift, pre=None, src=None, sname="X"):
        rsk = pre if pre is not None else rms_rstd(ct, src, sname)
        for ch in range(8):
            norm_chunk(ct, sec_scale, sec_shift, ch, rsk, src, sname)

    hr = lambda k, ct: ct.flat2(H, k)
    hk = lambda k, ct: [("H", k, ct.idx)]

    pre_rstd = {ct.idx: rms_rstd(ct) for ct in pass_tiles(0)}
    cast_mode[0] = "act"
    for k in range(8):
        adaln(8 + k)
        adaln(k)
        for ct in pass_tiles(0):
            norm_chunk(ct, 1, 0, k, pre_rstd[ct.idx])
    cast_mode[0] = "act"

    for p in range(NPASS):
        tiles = pass_tiles(p)
        last = p == NPASS - 1
        if p == 0:
            P.dma("pool", CS32, dram["cs_in"], writes=["CS32"])
            for ch in range(8):
                P.op("pool", lambda e, ch=ch: e.memset(UC[:, ch, 0:30], 0.0), writes=[("UC", ch, 0)])
                sv = UC[:, ch, 30 + TP:30 + TP + NS * 38].rearrange("p (s w) -> p s w", w=38)
                P.op("pool", lambda e, ch=ch, sv=sv: e.tensor_copy(out=sv[:, :, 0:30], in_=CS32[:, ch, :, :]), reads=["CS32"], writes=[("UC", ch, "s")])
                P.op("pool", lambda e, ch=ch: e.tensor_copy(out=CSO[:, ch, :, 0:22], in_=CS32[:, ch, :, 8:30]), reads=["CS32"], writes=[("CSO", ch)])
        else:
            for ch in range(8):
                P.op("pool", lambda e, ch=ch: e.tensor_copy(out=UC[:, ch, 0:30], in_=UH[:, ch, :]), reads=[("UH", ch)], writes=[("UC", ch, 0)])
            for ch in range(8):
                P.op("act", lambda e, ch=ch: e.activation(out=X[:, ch, 0:TP], in_=XN[:, ch, :], func=AF.Copy), reads=[("XN", ch, 0), ("XN", ch, 1)], writes=[("X", ch, 0), ("X", ch, 1)])
        def dg_build(ch):
            dsl = ch % 2
            P.op("dve", lambda e: e.tensor_tensor(out=DG[:, dsl, :, :], in0=IDB.unsqueeze(1).to_broadcast([128, KC, 128]),
                                                  in1=pv("wcd", ch * 31, 31).unsqueeze(2).to_broadcast([128, KC, 128]), op=ALU.mult),
                 reads=["IDB", "PV"], writes=[("DG", dsl)])

        dg_build(0)
        dg_build(1)
        for ch in range(8):
            ov = proj(("w_in", ch), 8, hr, hk, tiles)
            og = proj(("w_in", 8 + ch), 8, hr, hk, tiles)
            for (ct, bv), (_, bg) in zip(ov, og):
                tb, tk = nexttb()
                P.op("act", lambda e, ct=ct, bg=bg, tb=tb: e.activation(out=ct.tmp(tb), in_=psv(ct, bg), func=AF.Sigmoid), reads=[("ps", bg)], writes=[tk])
                P.op("dve", lambda e, ct=ct, bv=bv, ch=ch, tb=tb: e.tensor_tensor(out=ct.halo_w(UC, ch, 30), in0=psv(ct, bv), in1=ct.tmp(tb), op=ALU.mult),
                     reads=[("ps", bv), tk], writes=ct.k_w("UC", ch, 30))
            if p == 0:
                sv = UC[:, ch, 30 + TP:30 + TP + NS * 38].rearrange("p (s w) -> p s w", w=38)
                P.op("pool", lambda e, sv=sv, ch=ch: e.tensor_copy(out=CSO[:, ch, :, 22:30], in_=sv[:, :, 30:38]), reads=[("UC", ch, "s")], writes=[("CSO", ch)])
            if not last:
                P.op("pool", lambda e, ch=ch: e.tensor_copy(out=UH[:, ch, :], in_=UC[:, ch, TP:TP + 30]), reads=[("UC", ch, 2)], writes=[("UH", ch)])
            else:
                o, _ = STO["conf_p"]
                P.op("pool", lambda e, ch=ch, o=o: e.tensor_copy(out=ST[:, o + ch * 30:o + (ch + 1) * 30], in_=UC[:, ch, TP:TP + 30]),
                     reads=[("UC", ch, 2)], writes=[("ST", "conf_p", ch)])
        if p == 0:
            P.dma("pool", dram["cso"], CSO, reads=[("CSO", ch) for ch in range(8)])
        ada_pending = list(range(16, 48)) if p == 0 else []
        for ch in range(8):
            dsl = ch % 2
            if ch >= 2:
                dg_build(ch)
            for ct in tiles:
                b = psum()
                for j in range(KC):
                    P.op("pe", lambda e, ch=ch, j=j, dsl=dsl, ct=ct, b=b: e.matmul(PSB[b][:, 0:ct.cn], lhsT=DG[:, dsl, j, :], rhs=ct.halo_r(UC, ch, 30, j),
                                                                                  start=(j == 0), stop=(j == KC - 1)),
                         reads=[("DG", dsl)] + ct.k_r("UC", ch, 30), writes=[("ps", b)])
                P.op("act", lambda e, ch=ch, ct=ct, b=b: e.activation(out=ct.halo_r(UC, ch, 30, 0), in_=psv(ct, b), func=AF.Identity, bias=pv("bcd", ch), scale=1.0),
                     reads=[("ps", b), "PV"], writes=ct.k_s("UC", ch))
                for _ in range(0 if ct.s else 2):
                    if ada_pending:
                        adaln(ada_pending.pop(0))
        while ada_pending:
            adaln(ada_pending.pop(0))
        def mga(f):
            om = proj(("w_in", 36 + f), 8, hr, hk, tiles)
            for ct, bm in om:
                P.op("act", lambda e, ct=ct, bm=bm: e.activation(out=ct.flat(YA, f), in_=psv(ct, bm), func=AF.Tanh, scale=0.5), reads=[("ps", bm)], writes=[("YA", f, ct.idx)])

        fsplit = [[0, 1, 2], [3, 4, 5], [6, 7]] if len(tiles) == 3 else [[0, 1, 2, 3], [4, 5, 6, 7]]
        for ti, ct in enumerate(tiles):
            bs, bq = psum(), psum()
            for ch in range(8):
                cv = ct.halo_r(UC, ch, 30, 0)
                sq = SQ[:, ch % 2, 0:ct.cn]
                P.op("dve", lambda e, cv=cv, sq=sq, ct=ct: e.tensor_tensor(out=ct.v(sq), in0=cv, in1=cv, op=ALU.mult), reads=ct.k_s("UC", ch), writes=[("SQ", ch % 2)])
                P.op("pe", lambda e, cv=cv, ch=ch, ct=ct, bs=bs: e.matmul(PSB[bs][:, 0:ct.cn], lhsT=ONES, rhs=cv, start=(ch == 0), stop=(ch == 7)),
                     reads=ct.k_s("UC", ch) + ["ONES"], writes=[("ps", bs)])
                P.op("pe", lambda e, sq=sq, ch=ch, ct=ct, bq=bq: e.matmul(PSB[bq][:, 0:ct.cn], lhsT=ONES, rhs=sq, start=(ch == 0), stop=(ch == 7)),
                     reads=[("SQ", ch % 2), "ONES"], writes=[("ps", bq)])
            cn = ct.cn
            mean, mk = nextstat()
            rstd, rk = nextstat()
            P.op("act", lambda e, bs=bs, cn=cn, mean=mean: e.activation(out=mean[:, 0:cn], in_=PSB[bs][:, 0:cn], func=AF.Identity, scale=1.0 / D), reads=[("ps", bs)], writes=[mk])
            P.op("dve", lambda e, cn=cn, mean=mean, rstd=rstd: e.tensor_tensor(out=rstd[:, 0:cn], in0=mean[:, 0:cn], in1=mean[:, 0:cn], op=ALU.mult), reads=[mk], writes=[rk])
            P.op("dve", lambda e, cn=cn, bq=bq, rstd=rstd: e.scalar_tensor_tensor(out=rstd[:, 0:cn], in0=PSB[bq][:, 0:cn], scalar=1.0 / D, in1=rstd[:, 0:cn], op0=ALU.mult, op1=ALU.subtract),
                 reads=[("ps", bq), rk], writes=[rk])
            P.op("act", lambda e, cn=cn, rstd=rstd: e.activation(out=rstd[:, 0:cn], in_=rstd[:, 0:cn], func=AF.Ln, bias=EPSB, scale=1.0), reads=[rk, "EPSB"], writes=[rk])
            P.op("act", lambda e, cn=cn, rstd=rstd: e.activation(out=rstd[:, 0:cn], in_=rstd[:, 0:cn], func=AF.Exp, scale=-0.5), reads=[rk], writes=[rk])
            for f in fsplit[ti]:
                mga(f)
            for ch in range(8):
                cv = ct.halo_r(UC, ch, 30, 0)
                tb, tk = nexttb()
                P.op("dve", lambda e, cv=cv, ct=ct, tb=tb, mean=mean: e.tensor_tensor(out=ct.tmp(tb), in0=cv, in1=ct.tmp(mean), op=ALU.subtract), reads=ct.k_s("UC", ch) + [mk], writes=[tk])
                P.op("dve", lambda e, ct=ct, tb=tb, rstd=rstd: e.tensor_tensor(out=ct.tmp(tb), in0=ct.tmp(tb), in1=ct.tmp(rstd), op=ALU.mult), reads=[tk, rk], writes=[tk])
                P.op("act", lambda e, cv=cv, ct=ct, ch=ch, tb=tb: e.activation(out=cv, in_=ct.tmp(tb), func=AF.Silu, bias=pv("bln", ch), scale=pv("gln", ch)),
                     reads=[tk, "PV"], writes=ct.k_s("UC", ch))
        ar = lambda k, ct: ct.halo_r(UC, k, 30, 0)
        ak = lambda k, ct: ct.k_s("UC", k)
        for f in range(8):
            oy = proj(("w_co", f), 8, ar, ak, tiles)
            for ct, by in oy:
                P.op("dve", lambda e, ct=ct, by=by, f=f: e.scalar_tensor_tensor(out=ct.flat(YA, f), in0=ct.flat(YA, f), scalar=1.0, in1=psv(ct, by), op0=ALU.add, op1=ALU.mult),
                     reads=[("ps", by), ("YA", f, ct.idx)], writes=[("YA", f, ct.idx)])
        P.fence()
        if p == 0:
            o, _ = STO["rconv_s"]
            for j in range(10):
                P.op("pool", lambda e, j=j: e.memset(RX[:, j, 0:3], 0.0), writes=[("RX", j, 0)])
                sv = RX[:, j, 3 + TP:3 + TP + NS * 11].rearrange("p (s w) -> p s w", w=11)
                iv = ST[:, o + j * 48:o + (j + 1) * 48].rearrange("p (s w) -> p s w", w=3)
                P.op("pool", lambda e, sv=sv, iv=iv: e.tensor_copy(out=sv[:, :, 0:3], in_=iv), reads=[("ST", "rconv_s")], writes=[("RX", j, "s")])
        else:
            for j in range(10):
                P.op("pool", lambda e, j=j: e.tensor_copy(out=RX[:, j, 0:3], in_=RXH[:, j, :]), reads=[("RXH", j)], writes=[("RX", j, 0)])
        def b1(j):
            ox = proj(("w_in", 16 + j), 8, hr, hk, tiles)
            for ct, b in ox:
                P.op("act", lambda e, ct=ct, b=b: e.activation(out=ct.halo_w(RX, j, 3), in_=psv(ct, b), func=AF.Copy), reads=[("ps", b)], writes=ct.k_w("RX", j, 3))
            if p == 0:
                o, _ = STO["rconv_s"]
                sv = RX[:, j, 3 + TP:3 + TP + NS * 11].rearrange("p (s w) -> p s w", w=11)
                dv = ST[:, o + j * 48:o + (j + 1) * 48].rearrange("p (s w) -> p s w", w=3)
                P.op("pool", lambda e: e.tensor_copy(out=dv, in_=sv[:, :, 8:11]), reads=[("RX", j, "s")], writes=[("ST", "rconv_s")])
            if not last:
                P.op("pool", lambda e: e.tensor_copy(out=RXH[:, j, :], in_=RX[:, j, TP:TP + 3]), reads=[("RX", j, 2)], writes=[("RXH", j)])
            else:
                o, _ = STO["rconv_p"]
                P.op("pool", lambda e: e.tensor_copy(out=ST[:, o + j * 3:o + (j + 1) * 3], in_=RX[:, j, TP:TP + 3]), reads=[("RX", j, 2)], writes=[("ST", "rconv_p")])

        def b2(j):
            for ct in tiles:
                tb, tk = nexttb()
                P.op("dve", lambda e, ct=ct, tb=tb: e.tensor_scalar(out=ct.tmp(tb), in0=ct.halo_r(RX, j, 3, 0), scalar1=pv("wrc", j * 4 + 0), scalar2=pv("brc", j),
                                                                  op0=ALU.mult, op1=ALU.add), reads=ct.k_r("RX", j, 3) + ["PV"], writes=[tk])
                for tap in (1, 2):
                    P.op("dve", lambda e, ct=ct, tap=tap, tb=tb: e.scalar_tensor_tensor(out=ct.tmp(tb), in0=ct.halo_r(RX, j, 3, tap), scalar=pv("wrc", j * 4 + tap),
                                                                                       in1=ct.tmp(tb), op0=ALU.mult, op1=ALU.add),
                         reads=ct.k_r("RX", j, 3) + ["PV", tk], writes=[tk])
                P.op("dve", lambda e, ct=ct, tb=tb: e.scalar_tensor_tensor(out=ct.halo_r(RX, j, 3, 0), in0=ct.halo_r(RX, j, 3, 3), scalar=pv("wrc", j * 4 + 3),
                                                                          in1=ct.tmp(tb), op0=ALU.mult, op1=ALU.add),
                     reads=ct.k_r("RX", j, 3) + ["PV", tk], writes=ct.k_s("RX", j))

        xr = lambda k, ct: ct.halo_r(RX, k, 3, 0)

        def bset(j):
            o = 4 * (j % 2)
            return BT[o:o + 4], ["BT%d" % (o + ii) for ii in range(4)]

        NCP = TP + (NS * TS if p == 0 else 0)
        stile = [ct for ct in tiles if ct.s]

        def b3_p1(j):
            ncp = NCP
            (TR, TI, TAA, TM), names = bset(j)
            kl = KLIST[j]
            wr, wrk = wget(("w_rg", j), 3)
            wi, wik = wget(("w_ig", j), 3)
            for (w, wk, dst, dn, hbo) in ((wr, wrk, TR, names[0], 0), (wi, wik, TI, names[1], 10)):
                for ct in tiles:
                    b = psum()
                    if w is not None:
                        for t, k in enumerate(kl):
                            P.op("pe", lambda e, b=b, t=t, k=k, ct=ct, w=w: e.matmul(PSB[b][:, 0:ct.cn], lhsT=w[:, t, :], rhs=xr(k, ct), start=(t == 0), stop=(t == len(kl) - 1)),
                                 reads=[wk] + ct.k_s("RX", k), writes=[("ps", b)])
                    dk = (dn, ct.idx)
                    P.op("act", lambda e, ct=ct, b=b, dst=dst, hbo=hbo: e.activation(out=ct.flat(dst), in_=psv(ct, b), func=AF.Tanh, bias=HB[:, hbo + j:hbo + j + 1], scale=0.5),
                         reads=[("ps", b), "HB"], writes=[dk])
            allk = lambda n: [(n, ct.idx) for ct in tiles]
            P.op("act", lambda e: e.activation(out=TAA[:, 0:ncp], in_=TR[:, 0:ncp], func=AF.Exp, scale=CNEG[:, j:j + 1], bias=CNEG[:, j:j + 1]),
                 reads=allk(names[0]) + ["CNEG"], writes=allk(names[2]))
            P.op("act", lambda e: e.activation(out=TM[:, 0:ncp], in_=TR[:, 0:ncp], func=AF.Exp, scale=CNEG[:, 10 + j:11 + j], bias=CNEG[:, 10 + j:11 + j]),
                 reads=allk(names[0]) + ["CNEG2"], writes=allk(names[3]))

        def b3_sqrt(j):
            ncp = NCP
            (TR, TI, TAA, TM), names = bset(j)
            kM = [(names[3], ct.idx) for ct in tiles]
            P.op("act", lambda e: e.activation(out=TM[:, 0:ncp], in_=TM[:, 0:ncp], func=AF.Sqrt, bias=0.25, scale=-0.25), reads=kM, writes=kM)

        def b3_p2(j):
            ncp = NCP
            (TR, TI, TAA, TM), names = bset(j)
            allk = lambda n: [(n, ct.idx) for ct in tiles]
            kR, kI, kA, kM = [allk(n) for n in names]
            P.op("dve", lambda e: e.scalar_tensor_tensor(out=TI[:, 0:ncp], in0=TI[:, 0:ncp], scalar=1.0, in1=TM[:, 0:ncp], op0=ALU.add, op1=ALU.mult), reads=kI + kM, writes=kI)
            P.op("dve", lambda e: e.tensor_tensor(out=TI[:, 0:TP], in0=TI[:, 0:TP], in1=RX[:, j, 0:TP], op=ALU.mult),
                 reads=kI + [("RX", j, 0), ("RX", j, 1)], writes=kI)
            for ct in stile:
                kr, ki, ka, km = [(n, ct.idx) for n in names]
                P.op("dve", lambda e, ct=ct: e.tensor_tensor(out=ct.flat(TI), in0=ct.flat(TI), in1=xr(j, ct), op=ALU.mult), reads=[ki] + ct.k_s("RX", j), writes=[ki])
                o, _ = STO["h_s"]
                h0 = ST[:, o + j * 16:o + (j + 1) * 16]
                a0 = ct.flat(TAA)[:, :, 0]
                b0 = ct.flat(TI)[:, :, 0]
                P.op("dve", lambda e, a0=a0, h0=h0: e.tensor_tensor(out=FX, in0=a0, in1=h0, op=ALU.mult), reads=[ka, ("ST", "h_s")], writes=["FX"])
                P.op("dve", lambda e, b0=b0: e.tensor_tensor(out=b0, in0=b0, in1=FX, op=ALU.add), reads=[ki, "FX"], writes=[ki])
                P.op("dve", lambda e, a0=a0: e.memset(a0, 0.0), reads=["FX"], writes=[ka])
            P.op("dve", lambda e: e.tensor_tensor_scan(out=TR[:, 0:TP], data0=TAA[:, 0:TP], data1=TI[:, 0:TP], initial=HCAR[:, j:j + 1], op0=ALU.mult, op1=ALU.add),
                 reads=kA + kI + [("HCAR", j)], writes=kR)
            P.op("dve", lambda e: e.tensor_copy(out=HCAR[:, j:j + 1], in_=TR[:, TP - 1:TP]), reads=kR, writes=[("HCAR", j)])
            for ct in stile:
                kr, ki, ka, km = [(n, ct.idx) for n in names]
                o, _ = STO["h_s"]
                h0 = ST[:, o + j * 16:o + (j + 1) * 16]
                P.op("dve", lambda e, ct=ct: e.tensor_tensor_scan(out=ct.flat2(TR), data0=ct.flat2(TAA), data1=ct.flat2(TI), initial=0.0, op0=ALU.mult, op1=ALU.add),
                     reads=[ka, ki], writes=[kr])
                P.op("dve", lambda e, ct=ct, h0=h0: e.tensor_copy(out=h0, in_=ct.flat(TR)[:, :, TS - 1]), reads=[kr], writes=[("ST", "h_s")])
            if last:
                o, _ = STO["h_p"]
                P.op("dve", lambda e, o=o: e.tensor_copy(out=ST[:, o + j:o + j + 1], in_=HCAR[:, j:j + 1]), reads=[("HCAR", j)], writes=[("ST", "h_p")])
            hgk = [("HG", j, ct.idx) for ct in tiles]
            P.op("dve", lambda e: e.tensor_tensor(out=HG[:, j, 0:ncp], in0=TR[:, 0:ncp], in1=HG[:, j, 0:ncp], op=ALU.mult), reads=kR + hgk, writes=hgk)

        def b1g(j):
            og = proj(("w_in", 26 + j), 8, hr, hk, tiles)
            for ct, b in og:
                P.op("act", lambda e, ct=ct, b=b: e.activation(out=ct.flat(HG, j), in_=psv(ct, b), func=AF.Gelu_apprx_tanh), reads=[("ps", b)], writes=[("HG", j, ct.idx)])

        b1n, b2n = [0], [0]

        bgn = [0]

        def ensure_b1(upto, gupto=None):
            gupto = upto if gupto is None else gupto
            while b1n[0] <= min(9, upto) or bgn[0] <= min(9, gupto):
                if b1n[0] <= min(9, upto):
                    b1(b1n[0])
                    b1n[0] += 1
                if bgn[0] <= min(9, gupto):
                    b1g(bgn[0])
                    bgn[0] += 1

        def ensure_b2(upto):
            while b2n[0] <= min(9, upto):
                b2(b2n[0])
                b2n[0] += 1

        cast_mode[0] = "act"
        ensure_b1(4, 1)
        ensure_b2(4)
        for m in range(5):
            j0, j1 = 2 * m, 2 * m + 1
            b3_p1(j0)
            b3_sqrt(j0)
            b3_p1(j1)
            b3_sqrt(j1)
            ensure_b1(j1 + 5, j1 + 3)
            b3_p2(j0)
            b3_p2(j1)
            ensure_b2(j1 + 5)
        cast_mode[0] = "act"
        gr = lambda k, ct: ct.flat2(HG, k)
        gk = lambda k, ct: [("HG", k, ct.idx)]
        for f in range(8):
            oy = proj_multi([(("w_ro", f, 0), 5, 0), (("w_ro", f, 1), 5, 5)], gr, gk, tiles)
            om = proj(("w_in", 44 + f), 8, hr, hk, tiles)
            for (ct, by), (_, bm) in zip(oy, om):
                tb, tk = nexttb()
                P.op("act", lambda e, ct=ct, bm=bm, tb=tb: e.activation(out=ct.tmp(tb), in_=psv(ct, bm), func=AF.Sigmoid), reads=[("ps", bm)], writes=[tk])
                P.op("dve", lambda e, ct=ct, by=by, tb=tb: e.tensor_tensor(out=ct.tmp(tb), in0=psv(ct, by), in1=ct.tmp(tb), op=ALU.mult), reads=[("ps", by), tk], writes=[tk])
                P.op("dve", lambda e, ct=ct, f=f, tb=tb: e.scalar_tensor_tensor(out=ct.flat(YA, f), in0=ct.flat(YA, f), scalar=0.5, in1=ct.tmp(tb), op0=ALU.mult, op1=ALU.add),
                     reads=[tk, ("YA", f, ct.idx)], writes=[("YA", f, ct.idx)])
        P.fence()
        yr = lambda k, ct: ct.flat2(YA, k)
        yk = lambda k, ct: [("YA", k, ct.idx)]
        for f in range(8):
            oo = proj(("w_out", f), 8, yr, yk, tiles)
            for ct, b in oo:
                resid(ct, b, 2, f)
        for ct in tiles:
            norm_to_H(ct, 4, 3)
        NG = 24 // GF
        pairn = [0]

        def ffn_chunk(q, sl, c):
            if p == 0:
                o, _ = STO["ffn_s"]
                P.op("pool", lambda e: e.memset(UPB[:, sl, 0:2], 0.0), writes=[("UP", sl, 0)])
                sv = UPB[:, sl, 2 + TP:2 + TP + NS * 10].rearrange("p (s w) -> p s w", w=10)
                iv = ST[:, o + c * 32:o + (c + 1) * 32].rearrange("p (s w) -> p s w", w=2)
                P.op("pool", lambda e: e.tensor_copy(out=sv[:, :, 0:2], in_=iv), reads=[("ST", "ffn_s")], writes=[("UP", sl, "s")])
            else:
                P.op("pool", lambda e: e.tensor_copy(out=UPB[:, sl, 0:2], in_=UPH[:, c, :]), reads=[("UPH", c)], writes=[("UP", sl, 0)])
            ou = proj(("w_up", c), 8, hr, hk, tiles)
            for ct, b in ou:
                P.op("act", lambda e, ct=ct, b=b: e.activation(out=ct.halo_w(UPB, sl, 2), in_=psv(ct, b), func=AF.Copy), reads=[("ps", b)], writes=ct.k_w("UP", sl, 2))
            if p == 0:
                o, _ = STO["ffn_s"]
                sv = UPB[:, sl, 2 + TP:2 + TP + NS * 10].rearrange("p (s w) -> p s w", w=10)
                dv = ST[:, o + c * 32:o + (c + 1) * 32].rearrange("p (s w) -> p s w", w=2)
                P.op("pool", lambda e: e.tensor_copy(out=dv, in_=sv[:, :, 8:10]), reads=[("UP", sl, "s")], writes=[("ST", "ffn_s")])
            if not last:
                P.op("pool", lambda e: e.tensor_copy(out=UPH[:, c, :], in_=UPB[:, sl, TP:TP + 2]), reads=[("UP", sl, 2)], writes=[("UPH", c)])
            else:
                o, _ = STO["ffn_p"]
                P.op("pool", lambda e: e.tensor_copy(out=ST[:, o + c * 2:o + (c + 1) * 2], in_=UPB[:, sl, TP:TP + 2]), reads=[("UP", sl, 2)], writes=[("ST", "ffn_p")])

        def ffn_pair_proj(q, i):
            ub = (q % 2) * 2 * GF
            ffn_chunk(q, ub + i, q * GF + i)
            ffn_chunk(q, ub + GF + i, 24 + q * GF + i)

        def ffn_pair(q, i):
            ub = (q % 2) * 2 * GF
            gb = (q % 2) * GF
            slg, slv = ub + i, ub + GF + i
            cg, cv = q * GF + i, 24 + q * GF + i
            dk = pairn[0] % 2
            for (c, o3) in ((cg, 0), (cv, 3)):
                P.op("dve", lambda e, c=c, o3=o3: e.tensor_tensor(out=DG3[:, dk, o3:o3 + 3, :], in0=IDB.unsqueeze(1).to_broadcast([128, 3, 128]),
                                                                in1=pv("wfd", c * 3, 3).unsqueeze(2).to_broadcast([128, 3, 128]), op=ALU.mult),
                     reads=["IDB", "PV"], writes=[("DG3", dk, o3)])
            for ct in tiles:
                k = pairn[0] % 4
                pairn[0] += 1
                fa, na = FT[k], "FT%d" % k
                bg, bv = psum(), psum()
                for (sl, o3, b) in ((slg, 0, bg), (slv, 3, bv)):
                    for tap in range(3):
                        P.op("pe", lambda e, ct=ct, sl=sl, o3=o3, b=b, tap=tap: e.matmul(PSB[b][:, 0:ct.cn], lhsT=DG3[:, dk, o3 + tap, :], rhs=ct.halo_r(UPB, sl, 2, tap),
                                                                                      start=(tap == 0), stop=(tap == 2)),
                             reads=[("DG3", dk, o3)] + ct.k_r("UP", sl, 2), writes=[("ps", b)])
                P.op("act", lambda e, ct=ct, fa=fa, bg=bg: e.activation(out=ct.tmp(fa), in_=psv(ct, bg), func=AF.Gelu_apprx_tanh, bias=pv("bfd", cg), scale=1.0),
                     reads=[("ps", bg), "PV"], writes=[na])
                P.op("dve", lambda e, ct=ct, fa=fa, bv=bv: e.scalar_tensor_tensor(out=ct.flat(GV, gb + i), in0=psv(ct, bv), scalar=pv("bfd", cv), in1=ct.tmp(fa),
                                                                                op0=ALU.add, op1=ALU.mult),
                     reads=[("ps", bv), na, "PV"], writes=[("GV", gb + i, ct.idx)])

        def ffn_down(q):
            gb = (q % 2) * GF
            vr = lambda k, ct: ct.flat2(GV, gb + k)
            vk = lambda k, ct: [("GV", gb + k, ct.idx)]
            for f in range(8):
                od = proj(("w_dn", q, f), GF, vr, vk, tiles)
                for ct, b in od:
                    resid(ct, b, 5, f)

        pairs = [(q, i) for q in range(NG) for i in range(GF)]
        pending = []
        for n in range(len(pairs) + 2):
            if n < len(pairs):
                ffn_pair_proj(*pairs[n])
            for (q, rdy) in list(pending):
                if rdy <= n:
                    ffn_down(q)
                    pending.remove((q, rdy))
            if 1 <= n <= len(pairs):
                q, i = pairs[n - 1]
                ffn_pair(q, i)
                if i == GF - 1:
                    pending.append((q, n + 1))
        assert not pending
        if not last:
            P.fence()
            for ch in range(8):
                P.dma("sp", XN[:, ch, :], dram["xT"][:, ch, (p + 1) * TP:(p + 2) * TP], writes=[("XN", ch, 0), ("XN", ch, 1)])
        else:
            stk = [("ST", "conf_p", ch) for ch in range(8)] + [("ST", n) for n in ("rconv_s", "rconv_p", "h_s", "h_p", "ffn_s", "ffn_p")]
            P.dma("pool", dram["st"], ST, reads=stk)
        for ct in tiles:
            rs, rk = rms_rstd(ct)
            for ch in range(8):
                xk = ("X", ch, ct.idx)
                P.op("dve", lambda e, ct=ct, ch=ch, rs=rs: e.scalar_tensor_tensor(out=ct.flat2(X, ch), in0=ct.flat2(X, ch), scalar=pv("gf", ch), in1=ct.tmp2(rs), op0=ALU.mult, op1=ALU.mult),
                     reads=[xk, rk, "PV"], writes=[xk])
            col0 = T if ct.s else p * TP + ct.c0
            P.dma("pool", dram["yT"][:, :, col0:col0 + ct.cn], X[:, :, ct.c0:ct.c0 + ct.cn], reads=[("X", ch, ct.idx) for ch in range(8)])
        if not last:
            for ct in pass_tiles(p + 1):
                norm_to_H(ct, 1, 0, src=XN, sname="XN")
    return wreq


def make_nc():
    nc = bass.Bass("TRN2", target_bir_lowering=False)
    dram = {}

    def din(name, shape):
        dram[name] = nc.dram_tensor(name, list(shape), F32, kind="ExternalInput").ap()

    din("xT", [128, 8, T + NS * TS])
    din("cT", [128, 8, 17])
    din("cs_in", [128, 8, NS, 30])
    din("rs_in", [128, 10 * NS * 3])
    din("h_in", [128, 10 * NS])
    din("fs_in", [128, 48 * NS * 2])
    din("pv", [128, NPV])
    din("idn", [128, 128])
    din("w_ada", [48, 128, 8 * 128])
    din("w_in", [52, 128, 8 * 128])
    din("w_co", [8, 128, 8 * 128])
    din("w_rg", [10, 128, 3 * 128])
    din("w_ig", [10, 128, 3 * 128])
    din("w_ro", [8, 2, 128, 5 * 128])
    din("w_out", [8, 128, 8 * 128])
    din("w_up", [48, 128, 8 * 128])
    din("w_dn", [24 // GF, 8, 128, GF * 128])
    dram["yT"] = nc.dram_tensor("yT", [128, 8, T + NS * TS], F32, kind="ExternalOutput").ap()
    dram["st"] = nc.dram_tensor("st", [128, NST], F32, kind="ExternalOutput").ap()
    dram["cso"] = nc.dram_tensor("cso", [128, 8, NS, 30], F32, kind="ExternalOutput").ap()
    return nc, dram


_CACHE = {}


def get_nc():
    if "nc" not in _CACHE:
        nc0, dram0 = make_nc()
        wreq = build_program(nc0, Prog(nc0, dry=True), None, dram0)
        nc, dram = make_nc()
        P = Prog(nc)
        build_program(nc, P, wreq, dram)
        P.emit()
        _CACHE["nc"] = nc
    return _CACHE["nc"]


def fm(a, nch):
    a = np.asarray(a)
    lead = a.shape[:-1]
    a = a.reshape(lead + (nch, 128))
    nd = a.ndim
    perm = (nd - 1, nd - 2) + tuple(range(nd - 2))
    return np.ascontiguousarray(a.transpose(perm))


def slab(w, kc):
    K, Fd = w.shape
    a = w.reshape(kc, 128, Fd // 128, 128).transpose(2, 1, 0, 3)
    return np.ascontiguousarray(a).reshape(Fd // 128, 128, kc * 128)


def kernel(x_prompt, x_sample, state_conf_conv, state_rnn_conv, state_rnn_h, state_ffn_conv,
           c_prompt, c_sample, w_ada, b_ada, g_norm1, w_in, w_conf_dw, b_conf_dw, g_conf_ln,
           b_conf_ln, w_conf_out, w_rnn_conv, b_rnn_conv, w_rg, b_rg, w_ig, b_ig, lru_lambda,
           w_rnn_out, w_out, g_norm2, w_up, w_ffn_dw, b_ffn_dw, w_down, g_final):
    f32 = np.float32
    A = lambda v: np.asarray(v, dtype=f32)
    x_prompt, x_sample = A(x_prompt), A(x_sample)
    pvv = np.zeros((128, NPV), f32)

    def put(name, arr):
        w = arr.reshape(128, -1).shape[1]
        pvv[:, PVO[name]:PVO[name] + w] = arr.reshape(128, -1)

    put("g1", fm(A(g_norm1)[0], 8)); put("g2", fm(A(g_norm2)[0], 8)); put("gf", fm(A(g_final), 8))
    put("bcd", fm(A(b_conf_dw)[0], 8)); put("gln", fm(A(g_conf_ln)[0], 8)); put("bln", fm(A(b_conf_ln)[0], 8))
    put("wcd", fm(A(w_conf_dw)[0], 8).transpose(0, 1, 2))
    put("brc", fm(A(b_rnn_conv)[0], 10)); put("wrc", fm(A(w_rnn_conv)[0], 10))
    put("brg", fm(A(b_rg)[0], 10)); put("big", fm(A(b_ig)[0], 10)); put("lam", fm(A(lru_lambda)[0], 10))
    put("bfd", fm(A(b_ffn_dw)[0], 48))
    put("wfd", fm(A(w_ffn_dw)[0], 48))
    put("bada", fm(A(b_ada)[0], 48))
    idn = np.eye(128, dtype=f32)
    W_ada = slab(A(w_ada)[0], 8)
    W_in = slab(A(w_in)[0], 8)
    W_co = slab(A(w_conf_out)[0], 8)
    W_ro = np.ascontiguousarray(slab(A(w_rnn_out)[0], 10).reshape(8, 128, 2, 640).transpose(0, 2, 1, 3))
    W_out = slab(A(w_out)[0], 8)
    W_up = slab(A(w_up)[0], 8)
    wd = A(w_down)[0]
    NG = 24 // GF
    W_dn = np.ascontiguousarray(wd.reshape(NG, GF, 128, 8, 128).transpose(0, 3, 2, 1, 4)).reshape(NG, 8, 128, GF * 128)

    def gate_slab(wg):
        full = np.zeros((DR, DR), f32)
        for g in range(8):
            full[g * 160:(g + 1) * 160, g * 160:(g + 1) * 160] = wg[g]
        out = np.zeros((10, 128, 3, 128), f32)
        for j in range(10):
            for t, k in enumerate(KLIST[j]):
                out[j, :, t, :] = full[k * 128:(k + 1) * 128, j * 128:(j + 1) * 128]
        return out.reshape(10, 128, 3 * 128)

    W_rg = gate_slab(A(w_rg)[0])
    W_ig = gate_slab(A(w_ig)[0])
    shared = {"pv": pvv, "idn": idn, "w_ada": W_ada, "w_in": W_in, "w_co": W_co, "w_rg": W_rg, "w_ig": W_ig,
              "w_ro": W_ro, "w_out": W_out, "w_up": W_up, "w_dn": W_dn}
    in_maps = []
    scc, src, srh, sfc = A(state_conf_conv)[0], A(state_rnn_conv)[0], A(state_rnn_h)[0], A(state_ffn_conv)[0]
    cp, cs = A(c_prompt), A(c_sample)
    for i in range(NCORES):
        sl = slice(NS * i, NS * (i + 1))
        xp = fm(x_prompt[i], 8)
        xs = fm(x_sample[sl].reshape(NS * TS, D), 8)
        xT = np.ascontiguousarray(np.concatenate([xp, xs], axis=2))
        cT = np.ascontiguousarray(np.concatenate([fm(cs[sl], 8), fm(cp[i:i + 1], 8)], axis=2))
        m = dict(shared)
        m["xT"] = xT
        m["cT"] = cT
        m["cs_in"] = fm(scc[sl], 8)
        m["rs_in"] = fm(src[sl], 10).reshape(128, -1)
        m["h_in"] = fm(srh[sl], 10).reshape(128, -1)
        m["fs_in"] = fm(sfc[sl], 48).reshape(128, -1)
        in_maps.append(m)
    nc = get_nc()
    res = run_bass_kernel_spmd(nc, in_maps, core_ids=list(range(NCORES)))
    y_p = np.zeros((8, T, D), f32)
    y_s = np.zeros((128, TS, D), f32)
    conf_p = np.zeros((1, 8, 30, D), f32); rconv_p = np.zeros((1, 8, 3, DR), f32)
    h_p = np.zeros((1, 8, DR), f32); ffn_p = np.zeros((1, 8, 2, 2 * DFF), f32)
    conf_s = np.zeros((1, 128, 30, D), f32); rconv_s = np.zeros((1, 128, 3, DR), f32)
    h_s = np.zeros((1, 128, DR), f32); ffn_s = np.zeros((1, 128, 2, 2 * DFF), f32)

    def unfm(a):
        nd = a.ndim
        perm = tuple(range(2, nd)) + (1, 0)
        b = a.transpose(perm)
        return b.reshape(b.shape[:-2] + (b.shape[-2] * 128,))

    for i in range(NCORES):
        r = res.results[i]
        yT = np.asarray(r["yT"]).reshape(128, 8, T + NS * TS)
        y_p[i] = unfm(yT[:, :, 0:T])
        y_s[NS * i:NS * (i + 1)] = unfm(yT[:, :, T:]).reshape(NS, TS, D)
        stv = np.asarray(r["st"]).reshape(128, NST)
        sec = lambda n, shp: stv[:, STO[n][0]:STO[n][0] + STO[n][1]].reshape((128,) + shp)
        sl = slice(NS * i, NS * (i + 1))
        conf_p[0, i] = unfm(sec("conf_p", (8, 30)))
        rconv_p[0, i] = unfm(sec("rconv_p", (10, 3)))
        h_p[0, i] = unfm(sec("h_p", (10,)))
        ffn_p[0, i] = unfm(sec("ffn_p", (48, 2)))
        conf_s[0, sl] = unfm(np.asarray(r["cso"]).reshape(128, 8, NS, 30))
        rconv_s[0, sl] = unfm(sec("rconv_s", (10, NS, 3)))
        h_s[0, sl] = unfm(sec("h_s", (10, NS)))
        ffn_s[0, sl] = unfm(sec("ffn_s", (48, NS, 2)))
    return (y_p, y_s, conf_p, rconv_p, h_p, ffn_p, conf_s, rconv_s, h_s, ffn_s)
```
